# Optimizing a Trainium2 kernel written in Bass

```python
import math
import jax, jax.numpy as jnp
from jax import lax
import numpy as np

D_MODEL = 1024
BATCH = 8
SEQ = 4096
DEPTH = 1

SSD_HEADS = 8
SSD_HEAD_DIM = 64
SSD_INNER = SSD_HEADS * SSD_HEAD_DIM
SSD_GROUPS = 2
SSD_STATE = 128
SSD_CONV = 4
SSD_CHUNK = 128
SSD_XBC = SSD_INNER + 2 * SSD_GROUPS * SSD_STATE
MLA_HEADS = 8
MLA_NOPE = 64
MLA_ROPE = 32
MLA_QK = MLA_NOPE + MLA_ROPE
MLA_V = 64
MLA_Q_RANK = 384
MLA_KV_RANK = 256
ROPE_THETA = 10000.0
ATTN_BLOCK = 128
MIX_WIDTH = SSD_INNER + MLA_HEADS * MLA_V
IN_WIDTH = SSD_INNER + SSD_XBC + SSD_HEADS + MLA_Q_RANK + MLA_KV_RANK + MLA_ROPE
IN_SPLITS = (SSD_INNER,
             SSD_INNER + SSD_XBC,
             SSD_INNER + SSD_XBC + SSD_HEADS,
             SSD_INNER + SSD_XBC + SSD_HEADS + MLA_Q_RANK,
             SSD_INNER + SSD_XBC + SSD_HEADS + MLA_Q_RANK + MLA_KV_RANK)
MEM_TOKENS = 256
MEM_HEADS = 4
MEM_HEAD_DIM = D_MODEL // MEM_HEADS
D_FF = 4 * D_MODEL
LN_EPS = 1e-5
RMS_EPS = 1e-6
DEEPNORM_ALPHA = (2.0 * DEPTH) ** 0.25
DEEPNORM_BETA = (8.0 * DEPTH) ** -0.25

kernel_name = "hybrid_ssd_mla_memxattn_deepnorm_layer"


def layer_norm(x, g, b):
    xf = x.astype(jnp.float32)
    mu = jnp.mean(xf, axis=-1, keepdims=True)
    var = jnp.mean(jnp.square(xf - mu), axis=-1, keepdims=True)
    return ((xf - mu) * lax.rsqrt(var + LN_EPS) * g.astype(jnp.float32) + b.astype(jnp.float32)).astype(x.dtype)


def rms_norm(x, g):
    xf = x.astype(jnp.float32)
    ms = jnp.mean(jnp.square(xf), axis=-1, keepdims=True)
    return (xf * lax.rsqrt(ms + RMS_EPS) * g.astype(jnp.float32)).astype(x.dtype)


def grouped_rms_norm(y, g, groups):
    b, s, c = y.shape
    yg = y.reshape(b, s, groups, c // groups)
    yg = yg * lax.rsqrt(jnp.mean(jnp.square(yg), axis=-1, keepdims=True) + RMS_EPS)
    return yg.reshape(b, s, c) * g.astype(jnp.float32)


def apply_rope(x, cos, sin):
    half = x.shape[-1] // 2
    xf = x.astype(jnp.float32)
    x1, x2 = xf[..., :half], xf[..., half:]
    return jnp.concatenate([x1 * cos - x2 * sin, x2 * cos + x1 * sin], axis=-1).astype(x.dtype)


def causal_depthwise_conv(u, w, b):
    c = u.shape[-1]
    y = lax.conv_general_dilated(u, w[:, None, :].astype(u.dtype), window_strides=(1,),
                                 padding=[(SSD_CONV - 1, 0)],
                                 dimension_numbers=("NWC", "WIO", "NWC"),
                                 feature_group_count=c)
    return y + b


def segsum(a):
    t = a.shape[-1]
    aa = jnp.broadcast_to(a[..., :, None], a.shape + (t,))
    aa = jnp.where(jnp.tril(jnp.ones((t, t), dtype=bool), -1), aa, 0.0)
    ss = jnp.cumsum(aa, axis=-2)
    return jnp.where(jnp.tril(jnp.ones((t, t), dtype=bool)), ss, -jnp.inf)


def ssd_chunked_scan(x, dt, a_head, bm, cm):
    b, s, h, p = x.shape
    g, n = bm.shape[-2:]
    e = h // g
    L = SSD_CHUNK
    c = s // L
    xf = (x.astype(jnp.float32) * dt[..., None]).reshape(b, c, L, g, e, p)
    a = jnp.moveaxis((dt * a_head).reshape(b, c, L, g, e), 2, -1)
    bc = bm.astype(jnp.float32).reshape(b, c, L, g, n)
    cc = cm.astype(jnp.float32).reshape(b, c, L, g, n)
    a_cs = jnp.cumsum(a, axis=-1)
    decay_ls = jnp.exp(segsum(a))
    cb = jnp.einsum("bclgn,bcsgn->bcgls", cc, bc)
    y_diag = jnp.einsum("bcgls,bcgels,bcsgep->bclgep", cb, decay_ls, xf)
    decay_to_end = jnp.exp(a_cs[..., -1:] - a_cs)
    states = jnp.einsum("bclgn,bcgel,bclgep->bcgepn", bc, decay_to_end, xf)
    chunk_decay = jnp.exp(a_cs[..., -1])

    def step(carry, inp):
        st, dec = inp
        return carry * dec[..., None, None] + st, carry

    init = jnp.zeros((b, g, e, p, n), jnp.float32)
    _, prev = lax.scan(step, init, (jnp.moveaxis(states, 1, 0), jnp.moveaxis(chunk_decay, 1, 0)))
    prev = jnp.moveaxis(prev, 0, 1)
    y_off = jnp.einsum("bclgn,bcgepn,bcgel->bclgep", cc, prev, jnp.exp(a_cs))
    return (y_diag + y_off).reshape(b, s, h, p)


def causal_block_attention(q, k, v, scale):
    b, s, h, d = q.shape
    nb = s // ATTN_BLOCK
    qb = jnp.moveaxis(q.reshape(b, nb, ATTN_BLOCK, h, d), 1, 0)
    k_pos = jnp.arange(s)

    def one_block(args):
        q_blk, i = args
        q_pos = i * ATTN_BLOCK + jnp.arange(ATTN_BLOCK)
        sc = jnp.einsum("bqhd,bkhd->bhqk", q_blk, k).astype(jnp.float32) * scale
        sc = jnp.where(k_pos[None, :] <= q_pos[:, None], sc, -jnp.inf)
        pr = jax.nn.softmax(sc, axis=-1).astype(v.dtype)
        return jnp.einsum("bhqk,bkhd->bqhd", pr, v)

    out = lax.map(one_block, (qb, jnp.arange(nb)))
    return jnp.moveaxis(out, 0, 1).reshape(b, s, h, v.shape[-1])


def hybrid_mixer(h, cos, sin, w_in, conv_w, conv_b, dt_bias, a_log, d_skip, ssd_norm_g,
                 q_norm_g, w_q_up, kv_norm_g, w_kv_up, w_out):
    b, s, _ = h.shape
    proj = h @ w_in
    z, xbc, dt_raw, q_lat, kv_lat, k_r = jnp.split(proj, IN_SPLITS, axis=-1)
    xbc = jax.nn.silu(causal_depthwise_conv(xbc, conv_w, conv_b))
    xs, bm, cm = jnp.split(xbc, [SSD_INNER, SSD_INNER + SSD_GROUPS * SSD_STATE], axis=-1)
    xs = xs.reshape(b, s, SSD_HEADS, SSD_HEAD_DIM)
    bm = bm.reshape(b, s, SSD_GROUPS, SSD_STATE)
    cm = cm.reshape(b, s, SSD_GROUPS, SSD_STATE)
    dt = jax.nn.softplus(dt_raw.astype(jnp.float32) + dt_bias.astype(jnp.float32))
    a_head = -jnp.exp(a_log.astype(jnp.float32))
    y = ssd_chunked_scan(xs, dt, a_head, bm, cm) + xs.astype(jnp.float32) * d_skip.astype(jnp.float32)[:, None]
    y = y.reshape(b, s, SSD_INNER) * jax.nn.silu(z.astype(jnp.float32))
    y = grouped_rms_norm(y, ssd_norm_g, SSD_GROUPS).astype(h.dtype)
    q = (rms_norm(q_lat, q_norm_g) @ w_q_up).reshape(b, s, MLA_HEADS, MLA_QK)
    q = jnp.concatenate([q[..., :MLA_NOPE], apply_rope(q[..., MLA_NOPE:], cos, sin)], axis=-1)
    kv = (rms_norm(kv_lat, kv_norm_g) @ w_kv_up).reshape(b, s, MLA_HEADS, MLA_NOPE + MLA_V)
    k_pe = apply_rope(k_r[:, :, None, :], cos, sin)
    k = jnp.concatenate([kv[..., :MLA_NOPE],
                         jnp.broadcast_to(k_pe, (b, s, MLA_HEADS, MLA_ROPE))], axis=-1)
    v = kv[..., MLA_NOPE:]
    o = causal_block_attention(q, k, v, MLA_QK ** -0.5).reshape(b, s, MLA_HEADS * MLA_V)
    return jnp.concatenate([y, o], axis=-1) @ w_out


def memory_cross_attention(h, mem, w_q, w_k, w_v, w_o):
    b, s, _ = h.shape
    m = mem.shape[1]
    q = (h @ w_q).reshape(b, s, MEM_HEADS, MEM_HEAD_DIM)
    k = (mem @ w_k).reshape(b, m, MEM_HEADS, MEM_HEAD_DIM)
    v = (mem @ w_v).reshape(b, m, MEM_HEADS, MEM_HEAD_DIM)
    sc = jnp.einsum("bshd,bmhd->bhsm", q, k).astype(jnp.float32) * (MEM_HEAD_DIM ** -0.5)
    pr = jax.nn.softmax(sc, axis=-1).astype(v.dtype)
    o = jnp.einsum("bhsm,bmhd->bshd", pr, v).reshape(b, s, D_MODEL)
    return o @ w_o


def sq_relu_mlp(h, w_up, w_down):
    return jnp.square(jax.nn.relu(h @ w_up)) @ w_down


def setup_inputs(seed: int = 0) -> dict:
    key = jax.random.key(seed)
    ks = jax.random.split(key, 32)
    f32 = jnp.float32

    def w(k, shape, fan_in, scale=1.0):
        return jax.random.normal(k, shape, f32) * (fan_in ** -0.5) * scale

    def gain(k, shape):
        return 1.0 + 0.02 * jax.random.normal(k, shape, f32)

    def bias(k, shape):
        return 0.02 * jax.random.normal(k, shape, f32)

    x = jax.random.normal(ks[0], (BATCH, SEQ, D_MODEL), f32)
    mem = jax.random.normal(ks[1], (BATCH, MEM_TOKENS, D_MODEL), f32)
    start = jax.random.randint(ks[2], (BATCH, 1), 0, 4096, dtype=jnp.int32)
    positions = (start + jnp.arange(SEQ, dtype=jnp.int32)[None, :]).astype(jnp.int32)

    dt0 = jnp.exp(jax.random.uniform(ks[3], (DEPTH, SSD_HEADS), f32,
                                     minval=math.log(1e-3), maxval=math.log(1e-1)))
    dt_bias = dt0 + jnp.log(-jnp.expm1(-dt0))
    a_log = jnp.log(jax.random.uniform(ks[4], (DEPTH, SSD_HEADS), f32, minval=1.0, maxval=16.0))
    v_col = (jnp.arange(MLA_NOPE + MLA_V) >= MLA_NOPE)
    kv_scale = jnp.tile(jnp.where(v_col, DEEPNORM_BETA, 1.0), MLA_HEADS).astype(f32)

    return {
        "x": x,
        "mem": mem,
        "positions": positions,
        "ln_in_g": gain(ks[5], (D_MODEL,)),
        "ln_in_b": bias(ks[6], (D_MODEL,)),
        "w_in": w(ks[7], (DEPTH, D_MODEL, IN_WIDTH), D_MODEL),
        "conv_w": w(ks[8], (DEPTH, SSD_CONV, SSD_XBC), SSD_CONV),
        "conv_b": bias(ks[9], (DEPTH, SSD_XBC)),
        "dt_bias": dt_bias,
        "a_log": a_log,
        "d_skip": gain(ks[10], (DEPTH, SSD_HEADS)),
        "ssd_norm_g": gain(ks[11], (DEPTH, SSD_INNER)),
        "q_norm_g": gain(ks[12], (DEPTH, MLA_Q_RANK)),
        "w_q_up": w(ks[13], (DEPTH, MLA_Q_RANK, MLA_HEADS * MLA_QK), MLA_Q_RANK),
        "kv_norm_g": gain(ks[14], (DEPTH, MLA_KV_RANK)),
        "w_kv_up": w(ks[15], (DEPTH, MLA_KV_RANK, MLA_HEADS * (MLA_NOPE + MLA_V)), MLA_KV_RANK) * kv_scale,
        "w_mix_out": w(ks[16], (DEPTH, MIX_WIDTH, D_MODEL), MIX_WIDTH, DEEPNORM_BETA),
        "ln1_g": gain(ks[17], (DEPTH, D_MODEL)),
        "ln1_b": bias(ks[18], (DEPTH, D_MODEL)),
        "w_mem_q": w(ks[19], (DEPTH, D_MODEL, D_MODEL), D_MODEL),
        "w_mem_k": w(ks[20], (DEPTH, D_MODEL, D_MODEL), D_MODEL),
        "w_mem_v": w(ks[21], (DEPTH, D_MODEL, D_MODEL), D_MODEL, DEEPNORM_BETA),
        "w_mem_o": w(ks[22], (DEPTH, D_MODEL, D_MODEL), D_MODEL, DEEPNORM_BETA),
        "ln2_g": gain(ks[23], (DEPTH, D_MODEL)),
        "ln2_b": bias(ks[24], (DEPTH, D_MODEL)),
        "w_up": w(ks[25], (DEPTH, D_MODEL, D_FF), D_MODEL, DEEPNORM_BETA),
        "w_down": w(ks[26], (DEPTH, D_FF, D_MODEL), D_FF, DEEPNORM_BETA),
        "ln3_g": gain(ks[27], (DEPTH, D_MODEL)),
        "ln3_b": bias(ks[28], (DEPTH, D_MODEL)),
    }


def reference(x, mem, positions, ln_in_g, ln_in_b, w_in, conv_w, conv_b, dt_bias, a_log, d_skip,
              ssd_norm_g, q_norm_g, w_q_up, kv_norm_g, w_kv_up, w_mix_out, ln1_g, ln1_b,
              w_mem_q, w_mem_k, w_mem_v, w_mem_o, ln2_g, ln2_b, w_up, w_down, ln3_g, ln3_b):
    half = MLA_ROPE // 2
    inv_freq = jnp.power(ROPE_THETA, -jnp.arange(half, dtype=jnp.float32) / half)
    ang = positions.astype(jnp.float32)[..., None] * inv_freq
    cos = jnp.cos(ang)[:, :, None, :]
    sin = jnp.sin(ang)[:, :, None, :]

    h = layer_norm(x, ln_in_g, ln_in_b)
    for l in range(DEPTH):
        mix = hybrid_mixer(h, cos, sin, w_in[l], conv_w[l], conv_b[l], dt_bias[l], a_log[l],
                           d_skip[l], ssd_norm_g[l], q_norm_g[l], w_q_up[l], kv_norm_g[l],
                           w_kv_up[l], w_mix_out[l])
        h = layer_norm(DEEPNORM_ALPHA * h + mix, ln1_g[l], ln1_b[l])
        xa = memory_cross_attention(h, mem, w_mem_q[l], w_mem_k[l], w_mem_v[l], w_mem_o[l])
        h = layer_norm(DEEPNORM_ALPHA * h + xa, ln2_g[l], ln2_b[l])
        ff = sq_relu_mlp(h, w_up[l], w_down[l])
        h = layer_norm(DEEPNORM_ALPHA * h + ff, ln3_g[l], ln3_b[l])
    return h
```

```python
import math
import numpy as np
import concourse.bass as bass
import concourse.mybir as mybir
from concourse.bass_utils import run_bass_kernel_spmd

F32 = mybir.dt.float32
BF16 = mybir.dt.bfloat16
I32 = mybir.dt.int32
AF = mybir.ActivationFunctionType
ALU = mybir.AluOpType

S = 4096
D = 1024
NT = S // 128
ALPHA = 2.0 ** 0.25
LN_EPS = 1e-5
RMS_EPS = 1e-6
SC_MLA = 96.0 ** -0.5
SC_MEM = 256.0 ** -0.5
WINC = 2312
NBLK = 26
NPC = 99
NPR = 2072
ENGS = ["pe", "act", "dve", "pool", "sp"]


class Sched:
    SAME_ENGINE_SYNC = ("act", "dve", "pool")

    def __init__(self, nc, n_dma_sems=24, schedule=True):
        self.nc = nc
        self.prog = []
        self.tabs = {}
        self.sem = {e: nc.alloc_semaphore(name="s_" + e) for e in ENGS}
        self.dsem = [nc.alloc_semaphore(name="d%d" % i) for i in range(n_dma_sems)]
        self.pbi = 0
        self._cap = None
        self._done = set()
        self.schedule = schedule

    def capture(self, f):
        self._cap = []
        f()
        c = self._cap
        self._cap = None
        return c

    def replay(self, it):
        if it[0] == "op":
            self.op(it[1], it[2], it[3], it[4], it[5])
        else:
            self.dma(it[1], it[2], it[3], eng=it[4], cost=it[5])

    def mark(self, kind, name):
        if self._cap is not None:
            self._cap.append((kind, name))
        elif kind == "sig":
            self._done.add(name)

    def merge(self, streams):
        streams = [x for x in streams if x]
        idx = [0] * len(streams)
        done = self._done
        while True:
            best, bk = None, None
            for k, st in enumerate(streams):
                while idx[k] < len(st) and st[idx[k]][0] in ("sig", "await"):
                    kind, nm = st[idx[k]]
                    if kind == "sig":
                        done.add(nm)
                        idx[k] += 1
                    elif nm in done:
                        idx[k] += 1
                    else:
                        break
                if idx[k] < len(st) and st[idx[k]][0] not in ("sig", "await"):
                    r = (idx[k] + 0.5) / len(st)
                    if best is None or r < best:
                        best, bk = r, k
            if bk is None:
                if all(idx[k] >= len(st) for k, st in enumerate(streams)):
                    break
                blocked = [st[idx[k]] for k, st in enumerate(streams) if idx[k] < len(st)]
                assert any(b[0] == "await" and b[1] in done for b in blocked), ("merge deadlock", blocked)
                continue
            self.replay(streams[bk][idx[bk]])
            idx[bk] += 1

    def op(self, eng, fn, reads=(), writes=(), cost=300, tab=None):
        if self._cap is not None:
            self._cap.append(("op", eng, fn, list(reads), list(writes), (cost, tab)))
            return
        if isinstance(cost, tuple):
            cost, tab = cost
        self.tabs[len(self.prog)] = tab
        self.prog.append(("op", eng, fn, tuple(reads), tuple(writes), cost))

    def dma(self, fn, reads=(), writes=(), eng="sp", cost=4000):
        if self._cap is not None:
            self._cap.append(("dma", fn, list(reads), list(writes), eng, cost))
            return
        self.prog.append(("dma", eng, fn, tuple(reads), tuple(writes), cost))

    def barrier(self):
        self.prog.append(("barrier",))

    def pb(self):
        i = self.pbi % 8
        self.pbi += 1
        return i

    @staticmethod
    def _build_preds(ops):
        last_w, readers = {}, {}
        preds = []
        for i, (kind, eng, fn, reads, writes, cost) in enumerate(ops):
            p = set()
            for k in reads:
                w = last_w.get(k)
                if w is not None:
                    p.add(w)
                if isinstance(k, str) and k.startswith("pb"):
                    for r in readers.get(k, ()):
                        if ops[r][1] != eng:
                            p.add(r)
            for k in writes:
                w = last_w.get(k)
                if w is not None:
                    p.add(w)
                p.update(readers.get(k, ()))
            p.discard(i)
            preds.append(p)
            for k in reads:
                readers.setdefault(k, []).append(i)
            for k in writes:
                last_w[k] = i
                readers[k] = []
        return preds

    def _order(self, ops, preds, tabs=None):
        n = len(ops)
        if not self.schedule:
            return list(range(n))
        import heapq
        succs = [[] for _ in range(n)]
        npred = [len(p) for p in preds]
        for i, p in enumerate(preds):
            for j in p:
                succs[j].append(i)
        ready = {e: [] for e in ENGS}
        est = [0.0] * n
        fin = [0.0] * n
        free_at = {e: 0.0 for e in ENGS}
        for i in range(n):
            if npred[i] == 0:
                ready[ops[i][1]].append(i)
        order = []
        cur_tab = ["EO"]
        WINDOW = 1500
        oldest = 0
        scheduled = [False] * n
        while len(order) < n:
            while oldest < n and scheduled[oldest]:
                oldest += 1
            best = None
            for e in ENGS:
                fa = free_at[e]
                for i in ready[e]:
                    if i > oldest + WINDOW:
                        continue
                    st = est[i] if est[i] > fa else fa
                    if e == "act" and tabs is not None and tabs[i] is not None and cur_tab[0] not in tabs[i]:
                        st += 1300.0
                    key = (st, i)
                    if best is None or key < best[0]:
                        best = (key, e, i)
            if best is None:
                cand = [(i, e) for e in ENGS for i in ready[e]]
                i, e = min(cand)
                best = ((max(est[i], free_at[e]), i), e, i)
            (st, _), e, i = best
            ready[e].remove(i)
            kind, eng, fn, reads, writes, cost = ops[i]
            if e == "act" and tabs is not None and tabs[i] is not None and cur_tab[0] not in tabs[i]:
                cur_tab[0] = tabs[i][0]
            if kind == "dma":
                free_at[e] = st + 150.0
                fin[i] = st + cost
            else:
                free_at[e] = st + cost
                fin[i] = st + cost
            scheduled[i] = True
            order.append(i)
            for j in succs[i]:
                lat = 260.0 if ops[j][1] != eng or kind == "dma" else 150.0
                t = fin[i] + lat
                if t > est[j]:
                    est[j] = t
                npred[j] -= 1
                if npred[j] == 0:
                    ready[ops[j][1]].append(j)
        return order

    def emit(self):
        nc = self.nc
        self.barrier()
        segs, cur, segtabs, curt = [], [], [], []
        for gi, it in enumerate(self.prog):
            if it[0] == "barrier":
                segs.append(cur)
                segtabs.append(curt)
                cur, curt = [], []
            else:
                cur.append(it)
                curt.append(self.tabs.get(gi))
        q = {e: [] for e in ENGS}
        cnt = {e: 0 for e in ENGS}
        seen = {e: {} for e in ENGS}
        dcnt = [0] * len(self.dsem)
        nd = {"sp": 0, "pool": 0}
        vclock = {}
        for ops, tabs in zip(segs, segtabs):
            preds = self._build_preds(ops)
            order = self._order(ops, preds, tabs)
            tok = [None] * len(ops)
            for i in order:
                kind, eng, fn, reads, writes, cost = ops[i]
                waits = []
                need = {}
                for p in preds[i]:
                    k, v = tok[p]
                    if k == eng and eng not in self.SAME_ENGINE_SYNC:
                        continue
                    if need.get(k, 0) < v:
                        need[k] = v
                for k, v in sorted(need.items(), key=lambda kv: -kv[1]):
                    if seen[eng].get(k, 0) >= v:
                        continue
                    waits.append((k, v))
                    seen[eng][k] = v
                    for k2, v2 in vclock.get((k, v), {}).items():
                        if k2 != eng and seen[eng].get(k2, 0) < v2:
                            seen[eng][k2] = v2
                if kind == "op":
                    cnt[eng] += 1
                    tok[i] = (eng, cnt[eng])
                    q[eng].append((waits, fn, (eng, 1)))
                    vclock[tok[i]] = dict(seen[eng])
                else:
                    if eng == "sp":
                        di = nd["sp"] % 16
                        nd["sp"] += 1
                    else:
                        di = 16 + nd["pool"] % (len(self.dsem) - 16)
                        nd["pool"] += 1
                    key = "D%d" % di
                    if dcnt[di] > 0 and seen[eng].get(key, 0) < dcnt[di] * 16:
                        seen[eng][key] = dcnt[di] * 16
                        waits.append((key, dcnt[di] * 16))
                    dcnt[di] += 1
                    tok[i] = (key, dcnt[di] * 16)
                    q[eng].append((waits, fn, (key, 16)))
                    vclock[tok[i]] = dict(seen[eng])
            snap = [(e, cnt[e]) for e in ENGS if cnt[e] > 0]
            snap += [("D%d" % i, c * 16) for i, c in enumerate(dcnt) if c > 0]
            for e in ENGS:
                waits = []
                for k, v in snap:
                    if (k != e or e in self.SAME_ENGINE_SYNC) and seen[e].get(k, 0) < v:
                        seen[e][k] = v
                        waits.append((k, v))
                q[e].append((waits, None, None))

        def semof(k):
            return self.dsem[int(k[1:])] if k[0] == "D" else self.sem[k]
        engmap = {"pe": "tensor", "act": "scalar", "dve": "vector", "pool": "gpsimd", "sp": "sync"}
        with nc.Block() as block:
            for e in ENGS:
                def body(engine, e=e):
                    for waits, fn, inc in q[e]:
                        for k, v in waits:
                            engine.wait_ge(semof(k), v)
                        if fn is not None:
                            fn(engine).then_inc(semof(inc[0]), inc[1])
                getattr(block, engmap[e])(body)


def build(debug=False, stop_after=3, p3_chunks=8, p3_stage=99):
    nc = bass.Bass("TRN2", target_bir_lowering=False)
    s = Sched(nc)

    def din(name, shape, dt=F32):
        return nc.dram_tensor(name, list(shape), dt, kind="ExternalInput").ap()

    x_d = din("x", [S, D])
    mem_d = din("mem", [256, D])
    pos_d = din("pos", [1, S], I32)
    win_d = din("w_in_l", [128, 8, WINC])
    wq2_d = din("wq2", [128, 8 * 3 * 2 * 96])
    wkv2_d = din("wkv2", [128, 2 * 1024])
    wblk_d = din("wblk", [NBLK, 128, 4096])
    pcols_d = din("pcols", [128, NPC])
    prows_d = din("prows", [1, NPR])
    cst_d = din("cst", [128, 3, 128])
    out_d = nc.dram_tensor("out", [S, D], F32, kind="ExternalOutput").ap()
    wbf_d = nc.dram_tensor("wblk_bf", [NBLK, 128, 4096], BF16, kind="Internal").ap()
    ropeC_d = nc.dram_tensor("ropeC", [32, S], F32, kind="Internal").ap()
    ropeS_d = nc.dram_tensor("ropeS", [32, S], F32, kind="Internal").ap()
    dbg = {}

    def dbg_out(name, shape, dt=F32):
        dbg[name] = nc.dram_tensor("dbg_" + name, list(shape), dt, kind="ExternalOutput").ap()
        return dbg[name]

    def dd(name, ap, keys, dt=F32):
        if not debug:
            return
        d = dbg_out(name, list(ap.shape), dt)
        s.dma(lambda e: e.dma_start(out=d, in_=ap), list(keys), [("dbg", name)])

    ARENA_BYTES = 207 * 1024
    arena = nc.alloc_sbuf_tensor("arena", [128, ARENA_BYTES // 2], BF16).ap()

    class Carver:
        def __init__(self, base=0):
            self.off = base

        def take(self, shape, dt=F32):
            esz = 4 if dt in (F32, I32) else 2
            n = int(np.prod(shape[1:]))
            nbytes = (n * esz + 31) // 32 * 32
            a = arena[:, self.off // 2:(self.off + n * esz) // 2]
            self.off += nbytes
            assert self.off <= ARENA_BYTES, ("SBUF arena overflow", self.off)
            if dt != BF16:
                a = a.bitcast(dt)
            if len(shape) == 3:
                a = a.rearrange("p (a b) -> p a b", a=shape[1])
            elif len(shape) == 4:
                a = a.rearrange("p (a b c) -> p a b c", a=shape[1], b=shape[2])
            return a

    G = Carver(0)
    cst = G.take([128, 3, 128])
    ident, tri, ones = cst[:, 0, :], cst[:, 1, :], cst[:, 2, :]
    identb = G.take([128, 128], BF16)
    trib = G.take([128, 128], BF16)
    onesb = G.take([128, 128], BF16)
    pcols = G.take([128, NPC])
    agab = G.take([128, 48])
    prow = G.take([128, 24])
    Abc = G.take([128, 8])
    nhalf = G.take([128, 8])
    yT = G.take([128, 4, S], BF16)
    stt_ = [G.take([128, 2, 6]) for _ in range(4)]
    mv_ = G.take([128, 4, 2])
    sm_ = G.take([128, 64])
    GEND = G.off
    R = Carver(GEND)
    oT = R.take([128, 4, S], BF16)
    R_after_oT = R.off
    P1 = Carver(GEND)
    w_in = P1.take([128, 8, WINC], BF16)
    qnT = P1.take([128, 3, S], BF16)
    kvnT = P1.take([128, 2, S], BF16)
    KT0 = P1.take([128, S], BF16)
    P1_persist_end = P1.off

    pbank = [nc.alloc_psum_tensor("pb%d" % i, [128, 512], F32).ap() for i in range(8)]

    def pbk(i):
        return "pb%d" % i

    def fsz(ap):
        n = 1
        for d in ap.shape[1:]:
            n *= d
        return n

    def mm(out, lhsT, rhs, start, stop, reads, writes):
        n = fsz(rhs)
        c = 60 + max(n, 64) / 2.0
        if rhs.dtype == F32:
            c *= 4
        s.op("pe", lambda e: e.matmul(out, lhsT=lhsT, rhs=rhs, start=start, stop=stop), reads, writes, c)

    def tp(out, in_, idn, reads, writes):
        s.op("pe", lambda e: e.transpose(out=out, in_=in_, identity=idn), reads, writes, 130)

    def act(out, in_, func, reads, writes, **kw):
        c = 60 + 220 + fsz(in_) / 1.0 + (100 if not isinstance(kw.get("scale", 1.0), float) else 0) \
            + (100 if not isinstance(kw.get("bias", 0.0), float) else 0) + (100 if "accum_out" in kw else 0)
        tab = {AF.Tanh: ("EO",), AF.Ln: ("NLE",), AF.Exp: ("EO", "NLE"), AF.Sin: ("TRIG",), AF.Silu: ("SILU",),
               AF.Sqrt: ("SQRT",)}.get(func)
        s.op("act", lambda e: e.activation(out=out, in_=in_, func=func, **kw), reads, writes, c, tab)

    def ecost(eng, n, two=False):
        if eng == "pool":
            return 350 + 2.6 * n
        return 1.4 * (90 + (2.1 if two else 1.05) * n)

    def in_psum(ap):
        return str(ap.name).startswith("pb")

    def tt(eng, out, in0, in1, op, reads, writes):
        two = not (in_psum(in0) or in_psum(in1))
        c = ecost(eng, fsz(out), two=two) if op != ALU.pow else 600 + 170 * fsz(out)
        s.op(eng, lambda e: e.tensor_tensor(out=out, in0=in0, in1=in1, op=op), reads, writes, c)

    def ts(eng, out, in0, s1, s2, op0, op1, reads, writes):
        c = ecost(eng, fsz(out))
        if op1 is None:
            s.op(eng, lambda e: e.tensor_scalar(out=out, in0=in0, scalar1=s1, scalar2=None, op0=op0), reads, writes, c)
        else:
            s.op(eng, lambda e: e.tensor_scalar(out=out, in0=in0, scalar1=s1, scalar2=s2, op0=op0, op1=op1), reads, writes, c)

    def stt(out, in0, scalar, in1, op0, op1, reads, writes):
        s.op("dve", lambda e: e.scalar_tensor_tensor(out=out, in0=in0, scalar=scalar, in1=in1, op0=op0, op1=op1), reads, writes,
             ecost("dve", fsz(out), two=not (in_psum(in0) or in_psum(in1))))

    def cp(eng, out, in_, reads, writes):
        s.op(eng, lambda e: e.tensor_copy(out=out, in_=in_), reads, writes, ecost(eng, fsz(out)))

    def dma(out, in_, reads, writes, eng="sp"):
        nbytes = out.shape[0] * fsz(out) * 4
        s.dma(lambda e: e.dma_start(out=out, in_=in_), reads, writes, eng=eng, cost=4000 + nbytes / 60.0)

    def memset(eng, ap, val, writes):
        s.op(eng, lambda e: e.memset(ap, val), (), writes, ecost(eng, fsz(ap)))

    lnctr = [0]

    def ln_stats_a(src_halves, rkeys):
        i = lnctr[0] % 4
        lnctr[0] += 1
        st = stt_[i]
        stk = "bnst%d" % i
        mv = mv_[:, i, :]
        mvk = "mv%d" % i
        for h in range(2):
            s.op("dve", lambda e, h=h: e.bn_stats(out=st[:, h, :], in_=src_halves[h]), rkeys, [stk], 650)
        s.op("dve", lambda e: e.bn_aggr(out=mv, in_=st.rearrange("p a b -> p (a b)")), [stk], [mvk])
        return i

    def ln_stats_b(i):
        sck = "lnsc%d" % i
        ts("pool", sm_[:, 8 * i:8 * i + 1], mv_[:, i, 1:2], LN_EPS, 1.0, ALU.add, ALU.mult, ["mv%d" % i], [sck])
        tt("pool", sm_[:, 8 * i + 1:8 * i + 2], sm_[:, 8 * i:8 * i + 1], nhalf[:, 0:1], ALU.pow, [sck, "nhalf"], [sck])

    def ln_stats_c(i):
        sc = sm_[:, 8 * i:8 * i + 4]
        sck = "lnsc%d" % i
        ts("dve", sc[:, 2:3], mv_[:, i, 0:1], sc[:, 1:2], -1.0, ALU.mult, ALU.mult, ["mv%d" % i, sck], [sck])
        return sc[:, 1:2], sc[:, 2:3], sck

    def ln_stats(src_halves, rkeys):
        i = ln_stats_a(src_halves, rkeys)
        ln_stats_b(i)
        return ln_stats_c(i)

    dma(cst, cst_d, [], ["cst"])
    dma(pcols, pcols_d, [], ["pcols"])
    dma(prow, prows_d[:, 0:24].partition_broadcast(128), [], ["prow"])
    WG = [512, 1024, 1536, 0, WINC]
    WGB = [0, 512, 1024, 1536, WINC]

    def wkey(col0):
        g = max(i for i in range(4) if WGB[i] <= col0)
        return ("w_in", g)
    for g in (1, 2, 3, 0):
        dma(w_in[:, :, WGB[g]:WGB[g + 1]], win_d[:, :, WGB[g]:WGB[g + 1]], [], [("w_in", g)], eng="pool")
    cast_next = [0]

    def issue_casts(n):
        for _ in range(n):
            b = cast_next[0]
            if b < NBLK:
                dma(wbf_d[b], wblk_d[b], [], [("wbf", b)], eng="pool")
                cast_next[0] += 1
    memset("dve", nhalf, -0.5, ["nhalf"])
    cp("dve", identb, ident, ["cst"], ["identb"])
    cp("dve", trib, tri, ["cst"], ["trib"])
    cp("dve", onesb, ones, ["cst"], ["onesb"])
    ts("dve", agab, pcols[:, 0:48], ALPHA, None, ALU.mult, None, ["pcols"], ["agab"])
    act(Abc, prow[:, 8:16], AF.Exp, ["prow"], ["Abc"])
    ts("dve", Abc, Abc, -1.0, None, ALU.mult, None, ["Abc"], ["Abc"])

    T = Carver(P1_persist_end)
    pi_t = T.take([128, 512], I32)
    ang_t = T.take([128, 512])
    kk_t = T.take([128, 512])
    rc_t = T.take([128, 512])
    M_RND = 12582912.0
    C1 = 6.28125
    C2 = 2.0 * math.pi - C1
    PI_LO = 3.1415925
    RW = slice(64, 96)
    for c8 in range(8):
        cols = slice(c8 * 512, (c8 + 1) * 512)
        dma(pi_t[RW, :], pos_d[:, cols].partition_broadcast(32), [], ["pi_t"])
        ts("dve", ang_t[RW, :], pi_t[RW, :], pcols[RW, 97:98], None, ALU.mult, None, ["pi_t", "pcols"], ["ang_t"])
        ts("dve", kk_t[RW, :], ang_t[RW, :], 1.0 / (2 * math.pi), M_RND, ALU.mult, ALU.add, ["ang_t"], ["kk_t"])
        ts("dve", kk_t[RW, :], kk_t[RW, :], M_RND, None, ALU.subtract, None, ["kk_t"], ["kk_t"])
        stt(ang_t[RW, :], kk_t[RW, :], -C1, ang_t[RW, :], ALU.mult, ALU.add, ["kk_t", "ang_t"], ["ang_t"])
        stt(ang_t[RW, :], kk_t[RW, :], -C2, ang_t[RW, :], ALU.mult, ALU.add, ["kk_t", "ang_t"], ["ang_t"])
        ts("dve", rc_t[RW, :], ang_t[RW, :], math.pi / 2, None, ALU.add, None, ["ang_t"], ["rc_t"])
        ts("dve", kk_t[RW, :], rc_t[RW, :], math.pi, -2 * math.pi, ALU.is_gt, ALU.mult, ["rc_t"], ["kk_t"])
        tt("dve", rc_t[RW, :], rc_t[RW, :], kk_t[RW, :], ALU.add, ["rc_t", "kk_t"], ["rc_t"])
        ts("dve", rc_t[RW, :], rc_t[RW, :], PI_LO, -PI_LO, ALU.min, ALU.max, ["rc_t"], ["rc_t"])
        act(rc_t[RW, :], rc_t[RW, :], AF.Sin, ["rc_t"], ["rc_t"])
        dma(ropeC_d[:, cols], rc_t[RW, :], ["rc_t"], [("ropeC", c8)])
        ts("dve", ang_t[RW, :], ang_t[RW, :], PI_LO, -PI_LO, ALU.min, ALU.max, ["ang_t"], ["ang_t"])
        act(ang_t[RW, :], ang_t[RW, :], AF.Sin, ["ang_t"], ["ang_t"])
        ts("dve", ang_t[RW, :], ang_t[RW, :], pcols[RW, 98:99], None, ALU.mult, None, ["ang_t", "pcols"], ["ang_t"])
        dma(ropeS_d[:, cols], ang_t[RW, :], ["ang_t"], [("ropeS", c8)])

    CH1 = 256
    W = Carver(P1_persist_end + 4 * 2048 + 128)
    xt = [W.take([128, D]) for _ in range(2)]
    xn = W.take([128, D])
    h0T = [W.take([128, 8, CH1], BF16) for _ in range(2)]
    ub = [W.take([128, CH1 + 3]) for _ in range(2)]
    hal = W.take([128, 8, 4])
    acc = [W.take([128, CH1]) for _ in range(2)]
    th = [W.take([128, CH1]) for _ in range(2)]
    xsT = [W.take([128, 4, CH1]) for _ in range(2)]
    BCT = [W.take([128, 4, CH1], BF16) for _ in range(2)]
    lat = W.take([128, 5, CH1])
    sq = [W.take([128, CH1]) for _ in range(2)]
    rsb = [W.take([128, CH1]) for _ in range(2)]
    rcs = W.take([128, 2, CH1])
    rtmp = W.take([128, 2, CH1])
    zs = W.take([128, 512])
    dts2 = [W.take([128, 8, 8]) for _ in range(2)]
    Wt = W.take([128, 8, 128])
    Urow = W.take([128, 128])
    apad = W.take([128, 64])
    NEG4b = W.take([128, 512], BF16)
    NEGSEL = W.take([128, 8])
    Dm2 = [W.take([128, 8, 128]) for _ in range(2)]
    MT = W.take([128, 8, 128], BF16)
    Btok = W.take([128, 2, 128], BF16)
    xdt = W.take([128, 512], BF16)
    xdtd = W.take([128, 512], BF16)
    Sst = W.take([128, 512])
    Sbf = W.take([128, 512], BF16)
    y1 = W.take([128, 512])
    y2 = W.take([128, 512])
    ssq = W.take([128, 8])
    P1_END = W.off
    assert P1_END <= ARENA_BYTES, P1_END

    memset("pool", hal, 0.0, ["hal"])
    memset("pool", Sst, 0.0, ["Sst"])
    memset("pool", Sbf, 0.0, ["Sbf"])
    memset("pool", Wt, 0.0, ["Wt"])
    memset("pool", Urow, 0.0, ["Urow"])
    memset("pool", apad, 0.0, ["apad"])
    memset("pool", NEGSEL, 0.0, ["NEGSEL"])
    memset("dve", Urow[32:40, :], 1.0, ["Urow"])
    cp("dve", Wt[0:8, :, :], ident[0:8, 0:8].unsqueeze(2).broadcast_to([8, 8, 128]), ["cst", "Wt"], ["Wt"])
    ts("dve", NEGSEL[32:40, :], ident[0:8, 0:8], -1.0, None, ALU.mult, None, ["cst", "NEGSEL"], ["NEGSEL"])
    for q4 in range(4):
        ts("dve", NEG4b[:, q4 * 128:(q4 + 1) * 128], tri, -1.0, 30000.0, ALU.add, ALU.mult, ["cst"], ["NEG4b"])

    actr = [0]
    bctr = [0]
    sqc = [0]

    def pbA():
        b = actr[0] % 2
        actr[0] += 1
        return b

    def pbB():
        b = 5 + bctr[0] % 3
        bctr[0] += 1
        return b

    def load_x(t):
        dma(xt[t % 2], x_d[t * 128:(t + 1) * 128, :], [], [("xt", t % 2)])

    def ln0_tile(t, j, par):
        xk = ("xt", t % 2)
        rstd, nmr, sck = ln_stats([xt[t % 2][:, 0:512], xt[t % 2][:, 512:1024]], [xk])
        act(xn, xt[t % 2], AF.Identity, [xk, sck], ["xn"], scale=rstd, bias=nmr)
        if t + 2 < NT:
            load_x(t + 2)
        for half in range(2):
            b = pbA()
            for q in range(4):
                ft = half * 4 + q
                tp(pbank[b][:, q * 128:(q + 1) * 128], xn[:, ft * 128:(ft + 1) * 128], ident, ["xn", "cst"], [pbk(b)])
            for q in range(4):
                ft = half * 4 + q
                act(h0T[par][:, ft, j * 128:(j + 1) * 128], pbank[b][:, q * 128:(q + 1) * 128], AF.Identity,
                    [pbk(b), "pcols"], [("h0T", par)], scale=pcols[:, ft:ft + 1], bias=pcols[:, 8 + ft:9 + ft])

    def proj_fm(par, col0, M):
        b = pbA()
        for kt in range(8):
            mm(pbank[b][0:M, 0:CH1], w_in[:, kt, col0:col0 + M], h0T[par][:, kt, :], kt == 0, kt == 7,
               [wkey(col0), ("h0T", par)], [pbk(b)])
        return b

    def stage_A(c):
        par = c % 2
        gcols = slice(c * CH1, (c + 1) * CH1)
        if c >= 1:
            issue_casts(2)
        for j in range(2):
            ln0_tile(2 * c + j, j, par)
        s.mark("sig", ("h0T", c))
        c8 = (c * CH1) // 512
        dma(rcs[RW, 0, :], ropeC_d[:, gcols], [("ropeC", c8)], ["rcs"])
        dma(rcs[RW, 1, :], ropeS_d[:, gcols], [("ropeS", c8)], ["rcs"])
        for ct in range(8):
            b = proj_fm(par, 512 + ct * 128, 128)
            u = ub[ct % 2]
            uk = ("ub", ct % 2)
            a = acc[ct % 2]
            ak = ("acc", ct % 2)
            tht = th[ct % 2]
            thk = ("th", ct % 2)
            cp("pool", u[:, 0:3], hal[:, ct, 0:3], ["hal"], [uk])
            act(u[:, 3:3 + CH1], pbank[b][:, 0:CH1], AF.Copy, [pbk(b)], [uk])
            act(a, pbank[b][:, 0:CH1], AF.Identity, [pbk(b), "pcols"], [ak],
                scale=pcols[:, 48 + ct * 4 + 3:48 + ct * 4 + 4], bias=pcols[:, 80 + ct:81 + ct])
            for k in range(3):
                stt(a, u[:, k:k + CH1], pcols[:, 48 + ct * 4 + k:48 + ct * 4 + k + 1], a, ALU.mult, ALU.add,
                    [uk, ak, "pcols"], [ak])
            cp("pool", hal[:, ct, 0:3], u[:, CH1:CH1 + 3], [uk], ["hal"])
            act(tht, a, AF.Tanh, [ak], [thk], scale=0.5)
            ts("dve", tht, tht, 1.0, 0.5, ALU.add, ALU.mult, [thk], [thk])
            if ct < 4:
                tt("pool", xsT[par][:, ct, :], tht, a, ALU.mult, [thk, ak], [("xsT", par)])
            else:
                tt("pool", BCT[par][:, ct - 4, :], tht, a, ALU.mult, [thk, ak], [("BCT", par)])
        for (i0, n, gcol, dstT, nm) in ((0, 3, 92, qnT, "qnT"), (3, 2, 95, kvnT, "kvnT")):
            for i in range(n):
                b = proj_fm(par, 1544 + (i0 + i) * 128, 128)
                act(lat[:, i0 + i, :], pbank[b][:, 0:CH1], AF.Copy, [pbk(b)], [("lat", i0 + i)])
                si = sqc[0] % 2
                sqc[0] += 1
                act(sq[si], pbank[b][:, 0:CH1], AF.Square, [pbk(b)], [("sq", si)])
                mm(pbank[2][:, 0:CH1], ones, sq[si], i == 0, i == n - 1, ["cst", ("sq", si)], [pbk(2)])
            r = rsb[0 if i0 == 0 else 1]
            rk = ("rsb", i0)
            act(r, pbank[2][:, 0:CH1], AF.Ln, [pbk(2)], [rk], scale=1.0 / (128 * n), bias=RMS_EPS)
            act(r, r, AF.Exp, [rk], [rk], scale=-0.5)
            for i in range(n):
                stt(dstT[:, i, gcols], lat[:, i0 + i, :], pcols[:, gcol + i:gcol + i + 1], r, ALU.mult, ALU.mult,
                    [("lat", i0 + i), rk, "pcols"], [(nm, c)])
        b1 = proj_fm(par, 2120, 96)
        b2 = proj_fm(par, 2216, 96)
        tt("dve", rtmp[RW, 0, :], pbank[b1][RW, 0:CH1], rcs[RW, 0, :], ALU.mult, [pbk(b1), "rcs"], ["rtmp0"])
        tt("dve", rtmp[RW, 1, :], pbank[b2][RW, 0:CH1], rcs[RW, 1, :], ALU.mult, [pbk(b2), "rcs"], ["rtmp1"])
        tt("dve", KT0[RW, gcols], rtmp[RW, 0, :], rtmp[RW, 1, :], ALU.add, ["rtmp0", "rtmp1"], [("KT0r", c)])

    def part_I(t):
        c, j = t // 2, t % 2
        par = c % 2
        tp_ = t % 2
        dts, Dm = dts2[tp_], Dm2[tp_]
        dk = lambda nm: (nm, tp_)
        hk = ("h0T", par)
        H0 = h0T[par]
        tcols = slice(j * 128, (j + 1) * 128)
        s.mark("await", ("h0T", c))
        bs = 3
        for kt in range(8):
            mm(pbank[bs][:, 0:8], H0[:, kt, tcols], w_in[:, kt, 1536:1544], kt == 0, kt == 7, [hk, ("w_in", 3)], [pbk(bs)])
        tt("dve", dts[:, 0, :], pbank[bs][:, 0:8], prow[:, 0:8], ALU.add, [pbk(bs), "prow"], [dk("dts0")])
        act(dts[:, 0, :], dts[:, 0, :], AF.Exp, [dk("dts0")], [dk("dts0")])
        act(dts[:, 1, :], dts[:, 0, :], AF.Ln, [dk("dts0")], [dk("dt")], bias=1.0, scale=1.0)
        tt("dve", dts[:, 2, :], dts[:, 1, :], Abc, ALU.mult, [dk("dt"), "Abc"], [dk("a")])
        mm(pbank[bs][:, 8:16], tri, dts[:, 2, :], True, True, ["cst", dk("a")], [pbk(bs)])
        cp("dve", dts[:, 3, :], pbank[bs][:, 8:16], [pbk(bs)], [dk("acs")])
        act(dts[:, 4, :], dts[:, 3, :], AF.Exp, [dk("acs")], [dk("eacs")])
        mm(pbank[bs][:, 16:24], ones, dts[:, 2, :], True, True, ["cst", dk("a")], [pbk(bs)])
        tt("dve", dts[:, 5, :], pbank[bs][:, 16:24], dts[:, 3, :], ALU.subtract, [pbk(bs), dk("acs")], [dk("dte")])
        act(dts[:, 7, :], pbank[bs][:, 16:24], AF.Exp, [pbk(bs)], [dk("cd")])
        cp("dve", apad.rearrange("p (r c) -> p r c", c=32)[:, :, 0:8], dts[:, 2, :].unsqueeze(1).broadcast_to([128, 2, 8]),
           [dk("a"), "apad"], ["apad"])
        bq = 4
        mm(pbank[bq][0:40, 0:128], apad[:, 0:40], tri, True, True, ["apad", "cst"], [pbk(bq)])
        cp("dve", Urow[0:8, :], pbank[bq][0:8, 0:128], [pbk(bq), "Urow"], ["Urow"])
        tt("dve", Wt[32:40, :, :], pbank[bq][32:40, 0:128].unsqueeze(1).broadcast_to([8, 8, 128]),
           NEGSEL[32:40, :].unsqueeze(2).broadcast_to([8, 8, 128]), ALU.mult, [pbk(bq), "NEGSEL", "Wt"], ["Wt"])
        for hh in range(2):
            bd_ = 4 if hh == 0 else 3
            mm(pbank[bd_], identb, NEG4b, True, False, ["identb", "NEG4b"], [pbk(bd_)])
            for h in range(4 * hh, 4 * hh + 4):
                mm(pbank[bd_][:, (h % 4) * 128:(h % 4 + 1) * 128], Wt[0:40, h, :], Urow[0:40, :], False, h % 4 == 3,
                   ["Wt", "Urow"], [pbk(bd_)])
            act(Dm[:, 4 * hh:4 * hh + 4, :].rearrange("p a b -> p (a b)"), pbank[bd_], AF.Exp, [pbk(bd_)], [dk("Dm")])
        act(dts[:, 5, :], dts[:, 5, :], AF.Exp, [dk("dte")], [dk("dte")])
        tt("dve", dts[:, 6, :], dts[:, 1, :], dts[:, 5, :], ALU.mult, [dk("dt"), dk("dte")], [dk("wdt")])
        s.mark("sig", ("PI", t))

    def part_II(t):
        c, j = t // 2, t % 2
        par = c % 2
        tp_ = t % 2
        dts, Dm = dts2[tp_], Dm2[tp_]
        dk = lambda nm: (nm, tp_)
        hk, xk_, bk_ = ("h0T", par), ("xsT", par), ("BCT", par)
        H0, XS, BC = h0T[par], xsT[par], BCT[par]
        tcols = slice(j * 128, (j + 1) * 128)
        gt = slice(t * 128, (t + 1) * 128)
        s.mark("await", ("PI", t))
        if True:
            bz = pbB()
            for kt in range(8):
                mm(pbank[bz], H0[:, kt, tcols], w_in[:, kt, 0:512], kt == 0, kt == 7, [hk, ("w_in", 0)], [pbk(bz)])
            act(zs, pbank[bz], AF.Tanh, [pbk(bz)], ["zs"], scale=0.5)
            ts("dve", zs, zs, 1.0, 0.5, ALU.add, ALU.mult, ["zs"], ["zs"])
            tt("dve", zs, zs, pbank[bz], ALU.mult, ["zs", pbk(bz)], ["zs"])
            bx = pbB()
            for ct in range(4):
                tp(pbank[bx][:, ct * 128:(ct + 1) * 128], XS[:, ct, tcols], ident, [xk_, "cst"], [pbk(bx)])
            xs3 = pbank[bx].rearrange("p (h q) -> p h q", h=8)
            tt("dve", xdt.rearrange("p (h q) -> p h q", h=8), xs3, dts[:, 1, :].unsqueeze(2).broadcast_to([128, 8, 64]),
               ALU.mult, [pbk(bx), dk("dt")], ["xdt"])
            tt("dve", xdtd.rearrange("p (h q) -> p h q", h=8), xs3, dts[:, 6, :].unsqueeze(2).broadcast_to([128, 8, 64]),
               ALU.mult, [pbk(bx), dk("wdt")], ["xdtd"])
            tt("dve", y2.rearrange("p (h q) -> p h q", h=8), xs3, prow[:, 16:24].unsqueeze(2).broadcast_to([128, 8, 64]),
               ALU.mult, [pbk(bx), "prow"], ["y2"])
            bb = pbB()
            pbb = pbank[bb].bitcast(BF16)
            for g in range(2):
                tp(pbb[:, g * 128:(g + 1) * 128], BC[:, g, tcols], identb, [bk_, "identb"], [pbk(bb)])
            cp("dve", Btok.rearrange("p a b -> p (a b)"), pbb[:, 0:256], [pbk(bb)], ["Btok"])
            bc = pbB()
            for g in range(2):
                mm(pbank[bc][:, g * 128:(g + 1) * 128], BC[:, g, tcols], BC[:, 2 + g, tcols], True, True, [bk_], [pbk(bc)])
            for g in range(2):
                tt("dve", MT[:, 4 * g:4 * g + 4, :], Dm[:, 4 * g:4 * g + 4, :],
                   pbank[bc][:, g * 128:(g + 1) * 128].unsqueeze(1).broadcast_to([128, 4, 128]), ALU.mult,
                   [dk("Dm"), pbk(bc)], ["MT"])
            by = pbB()
            for h in range(8):
                mm(pbank[by][:, h * 64:(h + 1) * 64], MT[:, h, :], xdt[:, h * 64:(h + 1) * 64], True, True, ["MT", "xdt"], [pbk(by)])
            bo = pbB()
            for g in range(2):
                mm(pbank[bo][:, g * 256:(g + 1) * 256], BC[:, 2 + g, tcols], Sbf[:, g * 256:(g + 1) * 256], True, True,
                   [bk_, "Sbf"], [pbk(bo)])
            bd = pbB()
            for g in range(2):
                mm(pbank[bd][:, g * 256:(g + 1) * 256], Btok[:, g, :], xdtd[:, g * 256:(g + 1) * 256], True, True,
                   ["Btok", "xdtd"], [pbk(bd)])
            tt("dve", y1.rearrange("p (h q) -> p h q", h=8), pbank[bo].rearrange("p (h q) -> p h q", h=8),
               dts[:, 4, :].unsqueeze(2).broadcast_to([128, 8, 64]), ALU.mult, [pbk(bo), dk("eacs")], ["y1"])
            tt("dve", y1, y1, pbank[by], ALU.add, ["y1", pbk(by)], ["y1"])
            tt("pool", y1, y1, y2, ALU.add, ["y1", "y2"], ["y1"])
            tt("pool", y1, y1, zs, ALU.mult, ["y1", "zs"], ["y1"])
            tt("pool", Sst.rearrange("p (h q) -> p h q", h=8), Sst.rearrange("p (h q) -> p h q", h=8),
               dts[:, 7, :].unsqueeze(2).broadcast_to([128, 8, 64]), ALU.mult, ["Sst", dk("cd")], ["Sst"])
            tt("dve", Sst, Sst, pbank[bd], ALU.add, ["Sst", pbk(bd)], ["Sst"])
            cp("pool", Sbf, Sst, ["Sst"], ["Sbf"])
            for g in range(2):
                act(y2[:, g * 256:(g + 1) * 256], y1[:, g * 256:(g + 1) * 256], AF.Square, ["y1", "y2"], ["y2", "ssq"],
                    accum_out=ssq[:, g:g + 1])
            ts("pool", ssq[:, 2:4], ssq[:, 0:2], 1.0 / 256, RMS_EPS, ALU.mult, ALU.add, ["ssq"], ["ssq"])
            tt("pool", ssq[:, 4:6], ssq[:, 2:4], nhalf[:, 0:2], ALU.pow, ["ssq", "nhalf"], ["ssq"])
            for g in range(2):
                ts("dve", y2[:, g * 256:(g + 1) * 256], y1[:, g * 256:(g + 1) * 256], ssq[:, 4 + g:5 + g], None, ALU.mult, None,
                   ["y1", "ssq", "y2"], ["y2"])
            bt = pbB()
            for ct in range(4):
                tp(pbank[bt][:, ct * 128:(ct + 1) * 128], y2[:, ct * 128:(ct + 1) * 128], ident, ["y2", "cst"], [pbk(bt)])
            for ct in range(4):
                act(yT[:, ct, gt], pbank[bt][:, ct * 128:(ct + 1) * 128], AF.Identity, [pbk(bt), "pcols"],
                    [("yT", t)], scale=pcols[:, 88 + ct:89 + ct], bias=0.0)

    load_x(0)
    load_x(1)
    NCH1 = S // CH1
    stage_A(0)
    part_I(0)
    for c in range(NCH1):
        t0, t1 = 2 * c, 2 * c + 1
        sx = s.capture(lambda: stage_A(c + 1)) if c + 1 < NCH1 else []
        sy = s.capture(lambda: (part_II(t0), part_II(t1)))
        sz = s.capture(lambda: (part_I(t1), part_I(t0 + 2) if t0 + 2 < NT else None))
        s.merge([sx, sy, sz])

    issue_casts(NBLK)
    s.barrier()
    if debug:
        for nm, src, shp in (("yT", yT, [128, 4, S]), ("qnT", qnT, [128, 3, S]), ("kvnT", kvnT, [128, 2, S]), ("KT0", KT0, [128, S])):
            d = dbg_out(nm, shp, BF16)
            dma(d, src, [], [("dbg", nm)])
        s.barrier()
    if stop_after <= 1:
        s.emit()
        return nc, dbg

    P2 = Carver(P1_persist_end)
    KT1 = P2.take([128, S], BF16)
    wq2 = P2.take([128, 8, 3, 2 * 96], BF16)
    wkv2 = P2.take([128, 2, 1024], BF16)
    ropeC = P2.take([128, S])
    ropeS = P2.take([128, S])
    QT = [P2.take([128, 512], BF16) for _ in range(3)]
    PT = [P2.take([128, 512], BF16) for _ in range(6)]
    Vev = P2.take([128, 32, 72], BF16)
    Vod = P2.take([128, 32, 128], BF16)
    rt1 = P2.take([128, 512])
    rt2 = P2.take([128, 512])
    rcp = P2.take([128, 512])
    bcs = P2.take([128, 512])
    KT = [KT0, KT1]

    dma(wq2.rearrange("p a b c -> p (a b c)"), wq2_d, [], ["wq2"], eng="pool")
    dma(wkv2.rearrange("p a b -> p (a b)"), wkv2_d, [], ["wkv2"], eng="pool")
    dma(ropeC[RW, :], ropeC_d, [], ["ropeC"])
    dma(ropeS[RW, :], ropeS_d, [], ["ropeS"])
    cp("pool", KT1[RW, :], KT0[RW, :], [], [("KT", 1)])
    memset("dve", Vev[:, :, 64:65], 1.0, ["Vev"])
    memset("pool", Vod, 0.0, ["Vod"])
    memset("pool", Vod[:, :, 0:1], 1.0, ["Vod"])

    PS_S = [0, 1, 2, 3]
    PS_O = [4, 5]
    PS_X = [6, 7]
    sctr = [0]
    qctr = [0]
    pctr = [0]
    octr = [0]
    xctr = [0]
    rtc = [0]
    import collections
    bg = collections.deque()

    def xbank():
        b = PS_X[xctr[0] % 2]
        xctr[0] += 1
        return b

    def bg_pop(n=1):
        for _ in range(n):
            if bg:
                bg.popleft()()

    def head_params(h):
        par = h % 2
        return dict(par=par, KTh=KT[par], ktk=("KT", par), V=(Vev if par == 0 else Vod), vk=("Vev" if par == 0 else "Vod"),
                    voff=(0 if par == 0 else 64), Mv=(65 if par == 0 else 128),
                    orow=(slice(0, 64) if par == 0 else slice(64, 128)), drow=(64 if par == 0 else 0))

    def kv_units(h):
        hp = head_params(h)
        units = []
        for c8 in range(8):
            def u(c8=c8):
                cols = slice(c8 * 512, (c8 + 1) * 512)
                b = xbank()
                for kt in range(2):
                    mm(pbank[b][0:64, :], wkv2[:, kt, h * 128:h * 128 + 64], kvnT[:, kt, cols], kt == 0, kt == 1, ["wkv2"], [pbk(b)])
                cp("dve", hp["KTh"][0:64, cols], pbank[b][0:64, :], [pbk(b)], [hp["ktk"]])
            units.append(u)
        for b4 in range(4):
            def u(b4=b4):
                b = xbank()
                for bl in range(8):
                    blk = b4 * 8 + bl
                    for kt in range(2):
                        mm(pbank[b][:, bl * 64:(bl + 1) * 64], kvnT[:, kt, blk * 128:(blk + 1) * 128],
                           wkv2[:, kt, h * 128 + 64:h * 128 + 128], kt == 0, kt == 1, ["wkv2"], [pbk(b)])
                cp("dve", hp["V"][:, b4 * 8:(b4 + 1) * 8, hp["voff"]:hp["voff"] + 64], pbank[b].rearrange("p (a b) -> p a b", a=8),
                   [pbk(b)], [hp["vk"]])
            units.append(u)
        return units

    qslots = {}

    def q_unit(h, j):
        def u():
            qcols = slice(j * 512, (j + 1) * 512)
            ba = xbank()
            for kt in range(3):
                mm(pbank[ba][0:96, :], wq2[:, h, kt, 0:96], qnT[:, kt, qcols], kt == 0, kt == 2, ["wq2"], [pbk(ba)])
            qi = qctr[0] % 3
            qctr[0] += 1
            Q = QT[qi]
            qk = ("QT", qi)
            qslots[(h, j)] = (Q, qk)
            ts("dve", Q[0:64, :], pbank[ba][0:64, :], SC_MLA, None, ALU.mult, None, [pbk(ba)], [qk])
            stt(rt1[RW, :], pbank[ba][RW, :], SC_MLA, ropeC[RW, qcols], ALU.mult, ALU.mult, [pbk(ba), "ropeC"], ["rt1"])
            s.op("dve", lambda e, ba=ba: e.stream_shuffle(out=rt2[RW, :], in_=pbank[ba][RW, :],
                                                           mask=list(range(16, 32)) + list(range(0, 16))),
                 [pbk(ba)], ["rt2"], 650)
            stt(rt2[RW, :], rt2[RW, :], SC_MLA, ropeS[RW, qcols], ALU.mult, ALU.mult, ["rt2", "ropeS"], ["rt2"])
            tt("pool", Q[RW, :], rt1[RW, :], rt2[RW, :], ALU.add, ["rt1", "rt2"], [qk])
        return u

    def norm_unit(h, j, ob):
        hp = head_params(h)

        def u():
            qcols = slice(j * 512, (j + 1) * 512)
            drow, orow = hp["drow"], hp["orow"]
            o0 = orow.start
            for qd in range(2):
                s.op("dve", lambda e, qd=qd: e.stream_shuffle(out=bcs[o0 + 32 * qd:o0 + 32 * qd + 32, :],
                                                               in_=rcp[drow:drow + 32, :], mask=[0] * 32),
                     ["rcp"], ["bcs"], 650)
            tt("dve", oT[orow, h // 2, qcols], pbank[ob][orow, :], bcs[orow, :], ALU.mult, [pbk(ob), "bcs"], [("oT", h, j)])
        return u

    for u in kv_units(0):
        u()
    q_unit(0, 0)()
    order = [(h, j) for h in range(8) for j in range(8)]
    for idx, (h, j) in enumerate(order):
        hp = head_params(h)
        KTh, ktk, V, vk, Mv = hp["KTh"], hp["ktk"], hp["V"], hp["vk"], hp["Mv"]
        Q, qk = qslots[(h, j)]
        if idx + 1 < len(order):
            bg.append(q_unit(*order[idx + 1]))
        if h + 1 < 8 and j >= 5:
            ku = kv_units(h + 1)
            share = {5: ku[0:3], 6: ku[3:7], 7: ku[7:12]}[j]
            bg.extend(share)
        ob = PS_O[octr[0] % 2]
        octr[0] += 1
        nkb = 4 * j + 4
        sbanks = {}
        pslots = {}

        def emit_S(kb):
            n0 = max(0, kb - 4 * j) * 128
            sb_ = PS_S[sctr[0] % 4]
            sctr[0] += 1
            sbanks[kb] = sb_
            mm(pbank[sb_][:, n0:512], KTh[0:96, kb * 128:(kb + 1) * 128], Q[0:96, n0:512], True, True, [ktk, qk], [pbk(sb_)])

        emit_S(0)
        if nkb > 1:
            emit_S(1)
        for kb in range(nkb):
            n0 = max(0, kb - 4 * j) * 128
            sb_ = sbanks[kb]
            pi = pctr[0] % 6
            pctr[0] += 1
            P = PT[pi]
            pk = ("PT", pi)
            act(P[:, n0:512], pbank[sb_][:, n0:512], AF.Exp, [pbk(sb_)], [pk])
            if kb >= 4 * j:
                tt("pool", P[:, n0:n0 + 128], P[:, n0:n0 + 128], trib, ALU.mult, [pk, "trib"], [pk])
            if kb + 2 < nkb:
                emit_S(kb + 2)
            mm(pbank[ob][0:Mv, n0:512], V[:, kb, 0:Mv], P[:, n0:512], kb == 0, kb == nkb - 1, [vk, pk], [pbk(ob)])
            if kb >= 1:
                bg_pop(1)
        drow = hp["drow"]
        s.op("dve", lambda e, ob=ob, drow=drow: e.reciprocal(out=rcp[drow:drow + 1, :], in_=pbank[ob][drow:drow + 1, :]),
             [pbk(ob)], ["rcp"], 3400)
        bg_pop(len(bg))
        bg.append(norm_unit(h, j, ob))
    bg_pop(len(bg))

    s.barrier()
    if debug:
        d = dbg_out("oT", [128, 4, S], BF16)
        dma(d, oT, [], [("dbg", "oT")])
        s.barrier()
    if stop_after <= 2:
        s.emit()
        return nc, dbg

    P3 = Carver(R_after_oT)
    kmemT = P3.take([128, 8, 256], BF16)
    vmem = P3.take([128, 2, 1024], BF16)
    wr = [P3.take([128, 4096], BF16) for _ in range(3)]
    tb = [P3.take([128, D]) for _ in range(6)]
    tbc = [0]

    def take4():
        ids = [(tbc[0] + k) % 6 for k in range(4)]
        tbc[0] += 4
        return ids
    rn = tb
    hres = P3.take([128, 8, 512])
    bfA = P3.take([128, 8, 512], BF16)
    bfB = P3.take([128, 8, 512], BF16)
    aT = P3.take([128, 32, 512], BF16)
    PTm2 = [P3.take([128, 2, 512], BF16) for _ in range(2)]
    rden = P3.take([128, 512])
    sqf = [P3.take([128, 512]) for _ in range(2)]
    g3b = P3.take([128, 2, D])

    dma(g3b.rearrange("p a b -> p (a b)"), prows_d[:, 24:24 + 2048].partition_broadcast(128), [], ["g3b"])

    seq = [22, 23, 24, 25]
    for c in range(8):
        seq += list(range(0, 22))
    wpos = [0]
    wissued = [0]

    def w_issue():
        i = wissued[0]
        if i < len(seq):
            dma(wr[i % 3], wbf_d[seq[i]], [], [("wr", i % 3)])
            wissued[0] += 1

    def w_next():
        i = wpos[0]
        wpos[0] += 1
        while wissued[0] < min(len(seq), i + 3):
            w_issue()
        return wr[i % 3], ("wr", i % 3)

    w_issue()
    w_issue()

    memT = bfA
    for mb in range(2):
        dma(rn[mb], mem_d[mb * 128:(mb + 1) * 128, :], [], [("rn", mb)])
    for mb in range(2):
        for half in range(2):
            b = s.pb()
            for q in range(4):
                ft = half * 4 + q
                tp(pbank[b][:, q * 128:(q + 1) * 128], rn[mb][:, ft * 128:(ft + 1) * 128], ident, [("rn", mb)], [pbk(b)])
            for q in range(4):
                ft = half * 4 + q
                act(memT[:, ft, mb * 128:(mb + 1) * 128], pbank[b][:, q * 128:(q + 1) * 128], AF.Copy, [pbk(b)], ["bfA"])
    for blk in range(2):
        wk, wkk = w_next()
        wk3 = wk.rearrange("p (a b) -> p a b", a=8)
        for q in range(4):
            ft = blk * 4 + q
            b = s.pb()
            for kt in range(8):
                mm(pbank[b][:, 0:256], wk3[:, kt, q * 128:(q + 1) * 128], memT[:, kt, 0:256], kt == 0, kt == 7, [wkk, "bfA"], [pbk(b)])
            act(kmemT[:, ft, :], pbank[b][:, 0:256], AF.Copy, [pbk(b)], ["kmemT"])
    for blk in range(2):
        wv, wvk = w_next()
        wv3 = wv.rearrange("p (a b) -> p a b", a=8)
        for mb in range(2):
            b = s.pb()
            for kt in range(8):
                mm(pbank[b], memT[:, kt, mb * 128:(mb + 1) * 128], wv3[:, kt, :], kt == 0, kt == 7, [wvk, "bfA"], [pbk(b)])
            act(vmem[:, mb, blk * 512:(blk + 1) * 512], pbank[b], AF.Copy, [pbk(b)], ["vmem"])

    HK = [("hres", j) for j in range(4)]
    RK = [("rn", j) for j in range(4)]

    def tok_to_fm4(ids, gcol, bcol, agcol, abcol, dstb):
        for ft in range(8):
            b = s.pb()
            for j in range(4):
                tp(pbank[b][:, j * 128:(j + 1) * 128], tb[ids[j]][:, ft * 128:(ft + 1) * 128], ident, [("rn", ids[j]), "cst"], [pbk(b)])
            if dstb is not None:
                act(dstb[0][:, ft, :], pbank[b], AF.Identity, [pbk(b), "pcols"], [dstb[1]],
                    scale=pcols[:, gcol + ft:gcol + ft + 1], bias=pcols[:, bcol + ft:bcol + ft + 1])
            ts("dve", hres[:, ft, :], pbank[b], agab[:, agcol + ft:agcol + ft + 1], agab[:, abcol + ft:abcol + ft + 1],
               ALU.mult, ALU.add, [pbk(b), "agab"], HK)

    def ln_stage(gcol, bcol, agcol, abcol, dstb, final_c=None):
        bks = []
        for j in range(4):
            bj = []
            for half in range(2):
                b = s.pb()
                bj.append(b)
                for q in range(4):
                    ft = half * 4 + q
                    tp(pbank[b][:, q * 128:(q + 1) * 128], hres[:, ft, j * 128:(j + 1) * 128], ident, [("hres", j), "cst"], [pbk(b)])
            bks.append(bj)
        idx = [ln_stats_a([pbank[bks[j][0]], pbank[bks[j][1]]], [pbk(bks[j][0]), pbk(bks[j][1])]) for j in range(4)]
        for j in range(4):
            ln_stats_b(idx[j])
        st = [ln_stats_c(idx[j]) for j in range(4)]
        ids = take4()
        for j in range(4):
            r_, rk_ = tb[ids[j]], ("rn", ids[j])
            act(r_[:, 0:512], pbank[bks[j][0]], AF.Identity, [pbk(bks[j][0]), st[j][2]], [rk_],
                scale=st[j][0], bias=st[j][1])
            ts("dve", r_[:, 512:1024], pbank[bks[j][1]], st[j][0], st[j][1], ALU.mult, ALU.add,
               [pbk(bks[j][1]), st[j][2]], [rk_])
        if final_c is None:
            tok_to_fm4(ids, gcol, bcol, agcol, abcol, dstb)
        else:
            for j in range(4):
                t = final_c * 4 + j
                r_, rk_ = tb[ids[j]], ("rn", ids[j])
                tt("pool", r_, r_, g3b[:, 0, :], ALU.mult, [rk_, "g3b"], [rk_])
                tt("dve", r_, r_, g3b[:, 1, :], ALU.add, [rk_, "g3b"], [rk_])
                dma(out_d[t * 128:(t + 1) * 128, :], r_, [rk_], [("out", t)], eng="pool")

    def branch_fm(nblk_cols, rhs_fn, rkeys, ktn):
        for blk in range(2):
            wb, wbk = w_next()
            wb3 = wb.rearrange("p (a b) -> p a b", a=8)
            for q in range(4):
                ft = blk * 4 + q
                b = s.pb()
                for kt in range(ktn):
                    mm(pbank[b], wb3[:, kt, q * 128:(q + 1) * 128], rhs_fn(kt), kt == 0, kt == ktn - 1, [wbk] + rkeys, [pbk(b)])
                yield ft, b

    rctr = [0]
    for c in range(p3_chunks if p3_stage >= 0 else 0):
        ccols = slice(c * 512, (c + 1) * 512)
        ids = take4()
        for j in range(4):
            t = c * 4 + j
            r_, rk_ = tb[ids[j]], ("rn", ids[j])
            dma(r_, x_d[t * 128:(t + 1) * 128, :], [], [rk_], eng="pool")
            i = ln_stats_a([r_[:, 0:512], r_[:, 512:1024]], [rk_])
            ln_stats_b(i)
            rstd, nmr, sck = ln_stats_c(i)
            act(r_, r_, AF.Identity, [rk_, sck], [rk_], scale=rstd, bias=nmr)
        tok_to_fm4(ids, 0, 8, 0, 8, None)
        if p3_stage < 1:
            continue
        for ft, b in branch_fm(None, lambda kt: (yT if kt < 4 else oT)[:, kt % 4, ccols], [], 8):
            tt("dve", hres[:, ft, :], hres[:, ft, :], pbank[b], ALU.add, HK + [pbk(b)], HK)
        if p3_stage < 2:
            continue
        ln_stage(16, 24, 16, 24, (bfA, "bfA"))
        if p3_stage < 3:
            continue
        for blk in range(2):
            wb, wbk = w_next()
            wb3 = wb.rearrange("p (a b) -> p a b", a=8)
            for q in range(4):
                ft = blk * 4 + q
                b = s.pb()
                for kt in range(8):
                    mm(pbank[b], wb3[:, kt, q * 128:(q + 1) * 128], bfA[:, kt, :], kt == 0, kt == 7, [wbk, "bfA"], [pbk(b)])
                act(bfB[:, ft, :], pbank[b], AF.Copy, [pbk(b)], [("bfB", ft)], scale=SC_MEM)
        for mh in range(4):
            PTm = PTm2[mh % 2]
            pk_ = lambda mb, mh=mh: ("PTm", mh % 2, mb)
            bden = s.pb()
            for mb in range(2):
                b = s.pb()
                for i in range(2):
                    mm(pbank[b], kmemT[:, 2 * mh + i, mb * 128:(mb + 1) * 128], bfB[:, 2 * mh + i, :], i == 0, i == 1,
                       ["kmemT", ("bfB", 2 * mh + i)], [pbk(b)])
                act(PTm[:, mb, :], pbank[b], AF.Exp, [pbk(b)], [pk_(mb)])
            for mb in range(2):
                mm(pbank[bden], onesb, PTm[:, mb, :], mb == 0, mb == 1, ["onesb", pk_(mb)], [pbk(bden)])
            act(rden, pbank[bden], AF.Ln, [pbk(bden)], ["rden"])
            act(rden, rden, AF.Exp, ["rden"], ["rden"], scale=-1.0)
            for i in range(2):
                b = s.pb()
                for mb in range(2):
                    mm(pbank[b], vmem[:, mb, (2 * mh + i) * 128:(2 * mh + i + 1) * 128], PTm[:, mb, :], mb == 0, mb == 1,
                       ["vmem", pk_(mb)], [pbk(b)])
                tt("dve", bfB[:, 2 * mh + i, :], pbank[b], rden, ALU.mult, [pbk(b), "rden"], [("bfB", 2 * mh + i)])
        if p3_stage < 4:
            continue
        for ft, b in branch_fm(None, lambda kt: bfB[:, kt, :], [("bfB", k) for k in range(8)], 8):
            tt("dve", hres[:, ft, :], hres[:, ft, :], pbank[b], ALU.add, HK + [pbk(b)], HK)
        if p3_stage < 5:
            continue
        ln_stage(32, 40, 32, 40, (bfA, "bfA"))
        if p3_stage < 6:
            continue
        for fb in range(8):
            wb, wbk = w_next()
            wb3 = wb.rearrange("p (a b) -> p a b", a=8)
            for q in range(4):
                fft = fb * 4 + q
                b = s.pb()
                for kt in range(8):
                    mm(pbank[b], wb3[:, kt, q * 128:(q + 1) * 128], bfA[:, kt, :], kt == 0, kt == 7, [wbk, "bfA"], [pbk(b)])
                sf = sqf[fft % 2]
                sfk = ("sqf", fft % 2)
                act(sf, pbank[b], AF.Square, [pbk(b)], [sfk])
                stt(aT[:, fft, :], pbank[b], 0.0, sf, ALU.is_gt, ALU.mult, [pbk(b), sfk], [("aT", fft)])
        if p3_stage < 7:
            continue
        for ft in range(8):
            wb, wbk = w_next()
            wb3 = wb.rearrange("p (a b) -> p a b", a=32)
            b = s.pb()
            for kt in range(32):
                mm(pbank[b], wb3[:, kt, :], aT[:, kt, :], kt == 0, kt == 31, [wbk, ("aT", kt)], [pbk(b)])
            tt("dve", hres[:, ft, :], hres[:, ft, :], pbank[b], ALU.add, HK + [pbk(b)], HK)
        if p3_stage < 8:
            continue
        ln_stage(0, 0, 0, 0, None, final_c=c)

    s.emit()
    return nc, dbg


def prep_shared(inp):
    f32 = np.float32
    w_in = np.asarray(inp["w_in"][0], f32)
    perm = np.r_[16:32, 0:16]
    ext = np.concatenate([w_in, np.zeros((1024, 64), f32), w_in[:, 2184:2216][:, perm]], axis=1)
    assert ext.shape[1] == WINC
    w_in_l = np.ascontiguousarray(ext.reshape(8, 128, WINC).transpose(1, 0, 2))
    wq = np.asarray(inp["w_q_up"][0], f32)
    wq2 = np.zeros((384, 8, 2, 96), f32)
    for h in range(8):
        wq2[:, h, 0, :] = wq[:, h * 96:(h + 1) * 96]
        wq2[:, h, 1, 64:96] = wq[:, h * 96 + 64 + perm]
    wq2 = np.ascontiguousarray(wq2.reshape(3, 128, 8, 2, 96).transpose(1, 2, 0, 3, 4)).reshape(128, 8 * 3 * 2 * 96)
    wkv = np.asarray(inp["w_kv_up"][0], f32)
    wkv2 = np.ascontiguousarray(wkv.reshape(2, 128, 1024).transpose(1, 0, 2)).reshape(128, 2048)

    def blk_cols(Wm, i):
        return Wm[:, 512 * i:512 * (i + 1)].reshape(8, 128, 512).transpose(1, 0, 2).reshape(128, 4096)

    blocks = []
    for nm in ("w_mix_out", "w_mem_q", "w_mem_o"):
        Wm = np.asarray(inp[nm][0], f32)
        blocks += [blk_cols(Wm, 0), blk_cols(Wm, 1)]
    Wu = np.asarray(inp["w_up"][0], f32)
    blocks += [blk_cols(Wu, i) for i in range(8)]
    Wd = np.asarray(inp["w_down"][0], f32)
    blocks += [Wd[:, 128 * i:128 * (i + 1)].reshape(32, 128, 128).transpose(1, 0, 2).reshape(128, 4096) for i in range(8)]
    for nm in ("w_mem_k", "w_mem_v"):
        Wm = np.asarray(inp[nm][0], f32)
        blocks += [blk_cols(Wm, 0), blk_cols(Wm, 1)]
    wblk = np.ascontiguousarray(np.stack(blocks, axis=0))
    assert wblk.shape == (NBLK, 128, 4096)

    def fcol(v):
        v = np.asarray(v, f32).reshape(-1)
        return v.reshape(-1, 128).T

    pcols = np.zeros((128, NPC), f32)
    pcols[:, 0:8] = fcol(inp["ln_in_g"])
    pcols[:, 8:16] = fcol(inp["ln_in_b"])
    pcols[:, 16:24] = fcol(inp["ln1_g"][0])
    pcols[:, 24:32] = fcol(inp["ln1_b"][0])
    pcols[:, 32:40] = fcol(inp["ln2_g"][0])
    pcols[:, 40:48] = fcol(inp["ln2_b"][0])
    cw = np.asarray(inp["conv_w"][0], f32)
    pcols[:, 48:80] = cw.reshape(4, 8, 128).transpose(2, 1, 0).reshape(128, 32)
    pcols[:, 80:88] = fcol(inp["conv_b"][0])
    pcols[:, 88:92] = fcol(inp["ssd_norm_g"][0])
    pcols[:, 92:95] = fcol(inp["q_norm_g"][0])
    pcols[:, 95:97] = fcol(inp["kv_norm_g"][0])
    half = 16
    inv_freq = np.power(np.float32(10000.0), -np.arange(half, dtype=f32) / np.float32(half)).astype(f32)
    pcols[64:96, 97] = np.concatenate([inv_freq, inv_freq])
    pcols[64:96, 98] = np.concatenate([-np.ones(16, f32), np.ones(16, f32)])
    prows = np.zeros((1, NPR), f32)
    prows[0, 0:8] = np.asarray(inp["dt_bias"][0], f32)
    prows[0, 8:16] = np.asarray(inp["a_log"][0], f32)
    prows[0, 16:24] = np.asarray(inp["d_skip"][0], f32)
    prows[0, 24:1048] = np.asarray(inp["ln3_g"][0], f32)
    prows[0, 1048:2072] = np.asarray(inp["ln3_b"][0], f32)
    cst = np.stack([np.eye(128), np.triu(np.ones((128, 128))), np.ones((128, 128))], axis=1).astype(f32)
    return {"w_in_l": w_in_l, "wq2": wq2, "wkv2": wkv2, "wblk": wblk, "pcols": pcols, "prows": prows,
            "cst": np.ascontiguousarray(cst)}


def make_in_maps(inp, cores):
    shared = prep_shared(inp)
    maps = []
    for b in cores:
        m = dict(shared)
        m["x"] = np.ascontiguousarray(np.asarray(inp["x"][b], np.float32))
        m["mem"] = np.ascontiguousarray(np.asarray(inp["mem"][b], np.float32))
        m["pos"] = np.ascontiguousarray(np.asarray(inp["positions"][b], np.int32).reshape(1, S))
        maps.append(m)
    return maps


def kernel(**inputs):
    nc, _ = build(debug=False)
    in_maps = make_in_maps(inputs, list(range(8)))
    res = run_bass_kernel_spmd(nc, in_maps, core_ids=list(range(8)))
    return np.stack([np.asarray(r["out"], np.float32) for r in res.results], axis=0)
```

```python
import math
import numpy as np
import concourse.bass as bass
import concourse.mybir as mybir
from concourse.bass_utils import run_bass_kernel_spmd

F32 = mybir.dt.float32
BF16 = mybir.dt.bfloat16
I32 = mybir.dt.int32
AF = mybir.ActivationFunctionType
ALU = mybir.AluOpType

S = 4096
D = 1024
NT = S // 128
ALPHA = 2.0 ** 0.25
LN_EPS = 1e-5
RMS_EPS = 1e-6
SC_MLA = 96.0 ** -0.5
SC_MEM = 256.0 ** -0.5
WINC = 2312
NBLK = 26
NPC = 99
NPR = 2072
ENGS = ["pe", "act", "dve", "pool", "sp"]


class Sched:
    SAME_ENGINE_SYNC = ("act", "dve", "pool")

    def __init__(self, nc, n_dma_sems=24, schedule=True):
        self.nc = nc
        self.prog = []
        self.tabs = {}
        self.sem = {e: nc.alloc_semaphore(name="s_" + e) for e in ENGS}
        self.dsem = [nc.alloc_semaphore(name="d%d" % i) for i in range(n_dma_sems)]
        self.pbi = 0
        self._cap = None
        self._done = set()
        self.schedule = schedule

    def capture(self, f):
        self._cap = []
        f()
        c = self._cap
        self._cap = None
        return c

    def replay(self, it):
        if it[0] == "op":
            self.op(it[1], it[2], it[3], it[4], it[5])
        else:
            self.dma(it[1], it[2], it[3], eng=it[4], cost=it[5])

    def mark(self, kind, name):
        if self._cap is not None:
            self._cap.append((kind, name))
        elif kind == "sig":
            self._done.add(name)

    def merge(self, streams):
        streams = [x for x in streams if x]
        idx = [0] * len(streams)
        done = self._done
        while True:
            best, bk = None, None
            for k, st in enumerate(streams):
                while idx[k] < len(st) and st[idx[k]][0] in ("sig", "await"):
                    kind, nm = st[idx[k]]
                    if kind == "sig":
                        done.add(nm)
                        idx[k] += 1
                    elif nm in done:
                        idx[k] += 1
                    else:
                        break
                if idx[k] < len(st) and st[idx[k]][0] not in ("sig", "await"):
                    r = (idx[k] + 0.5) / len(st)
                    if best is None or r < best:
                        best, bk = r, k
            if bk is None:
                if all(idx[k] >= len(st) for k, st in enumerate(streams)):
                    break
                blocked = [st[idx[k]] for k, st in enumerate(streams) if idx[k] < len(st)]
                assert any(b[0] == "await" and b[1] in done for b in blocked), ("merge deadlock", blocked)
                continue
            self.replay(streams[bk][idx[bk]])
            idx[bk] += 1

    def op(self, eng, fn, reads=(), writes=(), cost=300, tab=None):
        if self._cap is not None:
            self._cap.append(("op", eng, fn, list(reads), list(writes), (cost, tab)))
            return
        if isinstance(cost, tuple):
            cost, tab = cost
        self.tabs[len(self.prog)] = tab
        self.prog.append(("op", eng, fn, tuple(reads), tuple(writes), cost))

    def dma(self, fn, reads=(), writes=(), eng="sp", cost=4000):
        if self._cap is not None:
            self._cap.append(("dma", fn, list(reads), list(writes), eng, cost))
            return
        self.prog.append(("dma", eng, fn, tuple(reads), tuple(writes), cost))

    def barrier(self):
        self.prog.append(("barrier",))

    def pb(self):
        i = self.pbi % 8
        self.pbi += 1
        return i

    @staticmethod
    def _build_preds(ops):
        last_w, readers = {}, {}
        preds = []
        for i, (kind, eng, fn, reads, writes, cost) in enumerate(ops):
            p = set()
            for k in reads:
                w = last_w.get(k)
                if w is not None:
                    p.add(w)
                if isinstance(k, str) and k.startswith("pb"):
                    for r in readers.get(k, ()):
                        if ops[r][1] != eng:
                            p.add(r)
            for k in writes:
                w = last_w.get(k)
                if w is not None:
                    p.add(w)
                p.update(readers.get(k, ()))
            p.discard(i)
            preds.append(p)
            for k in reads:
                readers.setdefault(k, []).append(i)
            for k in writes:
                last_w[k] = i
                readers[k] = []
        return preds

    def _order(self, ops, preds, tabs=None):
        n = len(ops)
        if not self.schedule:
            return list(range(n))
        import heapq
        succs = [[] for _ in range(n)]
        npred = [len(p) for p in preds]
        for i, p in enumerate(preds):
            for j in p:
                succs[j].append(i)
        ready = {e: [] for e in ENGS}
        est = [0.0] * n
        fin = [0.0] * n
        free_at = {e: 0.0 for e in ENGS}
        for i in range(n):
            if npred[i] == 0:
                ready[ops[i][1]].append(i)
        order = []
        cur_tab = ["EO"]
        WINDOW = 1500
        oldest = 0
        scheduled = [False] * n
        while len(order) < n:
            while oldest < n and scheduled[oldest]:
                oldest += 1
            best = None
            for e in ENGS:
                fa = free_at[e]
                for i in ready[e]:
                    if i > oldest + WINDOW:
                        continue
                    st = est[i] if est[i] > fa else fa
                    if e == "act" and tabs is not None and tabs[i] is not None and cur_tab[0] not in tabs[i]:
                        st += 1300.0
                    key = (st, i)
                    if best is None or key < best[0]:
                        best = (key, e, i)
            if best is None:
                cand = [(i, e) for e in ENGS for i in ready[e]]
                i, e = min(cand)
                best = ((max(est[i], free_at[e]), i), e, i)
            (st, _), e, i = best
            ready[e].remove(i)
            kind, eng, fn, reads, writes, cost = ops[i]
            if e == "act" and tabs is not None and tabs[i] is not None and cur_tab[0] not in tabs[i]:
                cur_tab[0] = tabs[i][0]
            if kind == "dma":
                free_at[e] = st + 150.0
                fin[i] = st + cost
            else:
                free_at[e] = st + cost
                fin[i] = st + cost
            scheduled[i] = True
            order.append(i)
            for j in succs[i]:
                lat = 260.0 if ops[j][1] != eng or kind == "dma" else 150.0
                t = fin[i] + lat
                if t > est[j]:
                    est[j] = t
                npred[j] -= 1
                if npred[j] == 0:
                    ready[ops[j][1]].append(j)
        return order

    def emit(self):
        nc = self.nc
        self.barrier()
        segs, cur, segtabs, curt = [], [], [], []
        for gi, it in enumerate(self.prog):
            if it[0] == "barrier":
                segs.append(cur)
                segtabs.append(curt)
                cur, curt = [], []
            else:
                cur.append(it)
                curt.append(self.tabs.get(gi))
        q = {e: [] for e in ENGS}
        cnt = {e: 0 for e in ENGS}
        seen = {e: {} for e in ENGS}
        dcnt = [0] * len(self.dsem)
        nd = {"sp": 0, "pool": 0}
        vclock = {}
        for ops, tabs in zip(segs, segtabs):
            preds = self._build_preds(ops)
            order = self._order(ops, preds, tabs)
            tok = [None] * len(ops)
            for i in order:
                kind, eng, fn, reads, writes, cost = ops[i]
                waits = []
                need = {}
                for p in preds[i]:
                    k, v = tok[p]
                    if k == eng and eng not in self.SAME_ENGINE_SYNC:
                        continue
                    if need.get(k, 0) < v:
                        need[k] = v
                for k, v in sorted(need.items(), key=lambda kv: -kv[1]):
                    if seen[eng].get(k, 0) >= v:
                        continue
                    waits.append((k, v))
                    seen[eng][k] = v
                    for k2, v2 in vclock.get((k, v), {}).items():
                        if k2 != eng and seen[eng].get(k2, 0) < v2:
                            seen[eng][k2] = v2
                if kind == "op":
                    cnt[eng] += 1
                    tok[i] = (eng, cnt[eng])
                    q[eng].append((waits, fn, (eng, 1)))
                    vclock[tok[i]] = dict(seen[eng])
                else:
                    if eng == "sp":
                        di = nd["sp"] % 16
                        nd["sp"] += 1
                    else:
                        di = 16 + nd["pool"] % (len(self.dsem) - 16)
                        nd["pool"] += 1
                    key = "D%d" % di
                    if dcnt[di] > 0 and seen[eng].get(key, 0) < dcnt[di] * 16:
                        seen[eng][key] = dcnt[di] * 16
                        waits.append((key, dcnt[di] * 16))
                    dcnt[di] += 1
                    tok[i] = (key, dcnt[di] * 16)
                    q[eng].append((waits, fn, (key, 16)))
                    vclock[tok[i]] = dict(seen[eng])
            snap = [(e, cnt[e]) for e in ENGS if cnt[e] > 0]
            snap += [("D%d" % i, c * 16) for i, c in enumerate(dcnt) if c > 0]
            for e in ENGS:
                waits = []
                for k, v in snap:
                    if (k != e or e in self.SAME_ENGINE_SYNC) and seen[e].get(k, 0) < v:
                        seen[e][k] = v
                        waits.append((k, v))
                q[e].append((waits, None, None))

        def semof(k):
            return self.dsem[int(k[1:])] if k[0] == "D" else self.sem[k]
        engmap = {"pe": "tensor", "act": "scalar", "dve": "vector", "pool": "gpsimd", "sp": "sync"}
        with nc.Block() as block:
            for e in ENGS:
                def body(engine, e=e):
                    for waits, fn, inc in q[e]:
                        for k, v in waits:
                            engine.wait_ge(semof(k), v)
                        if fn is not None:
                            fn(engine).then_inc(semof(inc[0]), inc[1])
                getattr(block, engmap[e])(body)


def build(debug=False, stop_after=3, p3_chunks=8, p3_stage=99):
    nc = bass.Bass("TRN2", target_bir_lowering=False)
    s = Sched(nc)

    def din(name, shape, dt=F32):
        return nc.dram_tensor(name, list(shape), dt, kind="ExternalInput").ap()

    x_d = din("x", [S, D])
    mem_d = din("mem", [256, D])
    pos_d = din("pos", [1, S], I32)
    win_d = din("w_in_l", [128, 8, WINC])
    wq2_d = din("wq2", [128, 8 * 3 * 2 * 96])
    wkv2_d = din("wkv2", [128, 2 * 1024])
    wblk_d = din("wblk", [NBLK, 128, 4096])
    pcols_d = din("pcols", [128, NPC])
    prows_d = din("prows", [1, NPR])
    cst_d = din("cst", [128, 3, 128])
    out_d = nc.dram_tensor("out", [S, D], F32, kind="ExternalOutput").ap()
    wbf_d = nc.dram_tensor("wblk_bf", [NBLK, 128, 4096], BF16, kind="Internal").ap()
    ropeC_d = nc.dram_tensor("ropeC", [32, S], F32, kind="Internal").ap()
    ropeS_d = nc.dram_tensor("ropeS", [32, S], F32, kind="Internal").ap()
    dbg = {}

    def dbg_out(name, shape, dt=F32):
        dbg[name] = nc.dram_tensor("dbg_" + name, list(shape), dt, kind="ExternalOutput").ap()
        return dbg[name]

    def dd(name, ap, keys, dt=F32):
        if not debug:
            return
        d = dbg_out(name, list(ap.shape), dt)
        s.dma(lambda e: e.dma_start(out=d, in_=ap), list(keys), [("dbg", name)])

    ARENA_BYTES = 207 * 1024
    arena = nc.alloc_sbuf_tensor("arena", [128, ARENA_BYTES // 2], BF16).ap()

    class Carver:
        def __init__(self, base=0):
            self.off = base

        def take(self, shape, dt=F32):
            esz = 4 if dt in (F32, I32) else 2
            n = int(np.prod(shape[1:]))
            nbytes = (n * esz + 31) // 32 * 32
            a = arena[:, self.off // 2:(self.off + n * esz) // 2]
            self.off += nbytes
            assert self.off <= ARENA_BYTES, ("SBUF arena overflow", self.off)
            if dt != BF16:
                a = a.bitcast(dt)
            if len(shape) == 3:
                a = a.rearrange("p (a b) -> p a b", a=shape[1])
            elif len(shape) == 4:
                a = a.rearrange("p (a b c) -> p a b c", a=shape[1], b=shape[2])
            return a

    G = Carver(0)
    cst = G.take([128, 3, 128])
    ident, tri, ones = cst[:, 0, :], cst[:, 1, :], cst[:, 2, :]
    identb = G.take([128, 128], BF16)
    trib = G.take([128, 128], BF16)
    onesb = G.take([128, 128], BF16)
    pcols = G.take([128, NPC])
    agab = G.take([128, 48])
    prow = G.take([128, 24])
    Abc = G.take([128, 8])
    nhalf = G.take([128, 8])
    yT = G.take([128, 4, S], BF16)
    stt_ = [G.take([128, 2, 6]) for _ in range(4)]
    mv_ = G.take([128, 4, 2])
    sm_ = G.take([128, 64])
    GEND = G.off
    R = Carver(GEND)
    oT = R.take([128, 4, S], BF16)
    R_after_oT = R.off
    P1 = Carver(GEND)
    w_in = P1.take([128, 8, WINC], BF16)
    qnT = P1.take([128, 3, S], BF16)
    kvnT = P1.take([128, 2, S], BF16)
    KT0 = P1.take([128, S], BF16)
    P1_persist_end = P1.off

    pbank = [nc.alloc_psum_tensor("pb%d" % i, [128, 512], F32).ap() for i in range(8)]

    def pbk(i):
        return "pb%d" % i

    def fsz(ap):
        n = 1
        for d in ap.shape[1:]:
            n *= d
        return n

    def mm(out, lhsT, rhs, start, stop, reads, writes):
        n = fsz(rhs)
        c = 60 + max(n, 64) / 2.0
        if rhs.dtype == F32:
            c *= 4
        s.op("pe", lambda e: e.matmul(out, lhsT=lhsT, rhs=rhs, start=start, stop=stop), reads, writes, c)

    def tp(out, in_, idn, reads, writes):
        s.op("pe", lambda e: e.transpose(out=out, in_=in_, identity=idn), reads, writes, 130)

    def act(out, in_, func, reads, writes, **kw):
        c = 60 + 220 + fsz(in_) / 1.0 + (100 if not isinstance(kw.get("scale", 1.0), float) else 0) \
            + (100 if not isinstance(kw.get("bias", 0.0), float) else 0) + (100 if "accum_out" in kw else 0)
        tab = {AF.Tanh: ("EO",), AF.Ln: ("NLE",), AF.Exp: ("EO", "NLE"), AF.Sin: ("TRIG",), AF.Silu: ("SILU",),
               AF.Sqrt: ("SQRT",)}.get(func)
        s.op("act", lambda e: e.activation(out=out, in_=in_, func=func, **kw), reads, writes, c, tab)

    def ecost(eng, n, two=False):
        if eng == "pool":
            return 350 + 2.6 * n
        return 1.4 * (90 + (2.1 if two else 1.05) * n)

    def in_psum(ap):
        return str(ap.name).startswith("pb")

    def tt(eng, out, in0, in1, op, reads, writes):
        two = not (in_psum(in0) or in_psum(in1))
        c = ecost(eng, fsz(out), two=two) if op != ALU.pow else 600 + 170 * fsz(out)
        s.op(eng, lambda e: e.tensor_tensor(out=out, in0=in0, in1=in1, op=op), reads, writes, c)

    def ts(eng, out, in0, s1, s2, op0, op1, reads, writes):
        c = ecost(eng, fsz(out))
        if op1 is None:
            s.op(eng, lambda e: e.tensor_scalar(out=out, in0=in0, scalar1=s1, scalar2=None, op0=op0), reads, writes, c)
        else:
            s.op(eng, lambda e: e.tensor_scalar(out=out, in0=in0, scalar1=s1, scalar2=s2, op0=op0, op1=op1), reads, writes, c)

    def stt(out, in0, scalar, in1, op0, op1, reads, writes):
        s.op("dve", lambda e: e.scalar_tensor_tensor(out=out, in0=in0, scalar=scalar, in1=in1, op0=op0, op1=op1), reads, writes,
             ecost("dve", fsz(out), two=not (in_psum(in0) or in_psum(in1))))

    def cp(eng, out, in_, reads, writes):
        s.op(eng, lambda e: e.tensor_copy(out=out, in_=in_), reads, writes, ecost(eng, fsz(out)))

    def dma(out, in_, reads, writes, eng="sp"):
        nbytes = out.shape[0] * fsz(out) * 4
        s.dma(lambda e: e.dma_start(out=out, in_=in_), reads, writes, eng=eng, cost=4000 + nbytes / 60.0)

    def memset(eng, ap, val, writes):
        s.op(eng, lambda e: e.memset(ap, val), (), writes, ecost(eng, fsz(ap)))

    lnctr = [0]

    def ln_stats_a(src_halves, rkeys):
        i = lnctr[0] % 4
        lnctr[0] += 1
        st = stt_[i]
        stk = "bnst%d" % i
        mv = mv_[:, i, :]
        mvk = "mv%d" % i
        for h in range(2):
            s.op("dve", lambda e, h=h: e.bn_stats(out=st[:, h, :], in_=src_halves[h]), rkeys, [stk], 650)
        s.op("dve", lambda e: e.bn_aggr(out=mv, in_=st.rearrange("p a b -> p (a b)")), [stk], [mvk])
        return i

    def ln_stats_b(i):
        sck = "lnsc%d" % i
        ts("pool", sm_[:, 8 * i:8 * i + 1], mv_[:, i, 1:2], LN_EPS, 1.0, ALU.add, ALU.mult, ["mv%d" % i], [sck])
        tt("pool", sm_[:, 8 * i + 1:8 * i + 2], sm_[:, 8 * i:8 * i + 1], nhalf[:, 0:1], ALU.pow, [sck, "nhalf"], [sck])

    def ln_stats_c(i):
        sc = sm_[:, 8 * i:8 * i + 4]
        sck = "lnsc%d" % i
        ts("dve", sc[:, 2:3], mv_[:, i, 0:1], sc[:, 1:2], -1.0, ALU.mult, ALU.mult, ["mv%d" % i, sck], [sck])
        return sc[:, 1:2], sc[:, 2:3], sck

    def ln_stats(src_halves, rkeys):
        i = ln_stats_a(src_halves, rkeys)
        ln_stats_b(i)
        return ln_stats_c(i)

    dma(cst, cst_d, [], ["cst"])
    dma(pcols, pcols_d, [], ["pcols"])
    dma(prow, prows_d[:, 0:24].partition_broadcast(128), [], ["prow"])
    WG = [512, 1024, 1536, 0, WINC]
    WGB = [0, 512, 1024, 1536, WINC]

    def wkey(col0):
        g = max(i for i in range(4) if WGB[i] <= col0)
        return ("w_in", g)
    for g in (1, 2, 3, 0):
        dma(w_in[:, :, WGB[g]:WGB[g + 1]], win_d[:, :, WGB[g]:WGB[g + 1]], [], [("w_in", g)], eng="pool")
    cast_next = [0]

    def issue_casts(n):
        for _ in range(n):
            b = cast_next[0]
            if b < NBLK:
                dma(wbf_d[b], wblk_d[b], [], [("wbf", b)], eng="pool")
                cast_next[0] += 1
    memset("dve", nhalf, -0.5, ["nhalf"])
    cp("dve", identb, ident, ["cst"], ["identb"])
    cp("dve", trib, tri, ["cst"], ["trib"])
    cp("dve", onesb, ones, ["cst"], ["onesb"])
    ts("dve", agab, pcols[:, 0:48], ALPHA, None, ALU.mult, None, ["pcols"], ["agab"])
    act(Abc, prow[:, 8:16], AF.Exp, ["prow"], ["Abc"])
    ts("dve", Abc, Abc, -1.0, None, ALU.mult, None, ["Abc"], ["Abc"])

    T = Carver(P1_persist_end)
    pi_t = T.take([128, 512], I32)
    ang_t = T.take([128, 512])
    kk_t = T.take([128, 512])
    rc_t = T.take([128, 512])
    M_RND = 12582912.0
    C1 = 6.28125
    C2 = 2.0 * math.pi - C1
    PI_LO = 3.1415925
    RW = slice(64, 96)
    for c8 in range(8):
        cols = slice(c8 * 512, (c8 + 1) * 512)
        dma(pi_t[RW, :], pos_d[:, cols].partition_broadcast(32), [], ["pi_t"])
        ts("dve", ang_t[RW, :], pi_t[RW, :], pcols[RW, 97:98], None, ALU.mult, None, ["pi_t", "pcols"], ["ang_t"])
        ts("dve", kk_t[RW, :], ang_t[RW, :], 1.0 / (2 * math.pi), M_RND, ALU.mult, ALU.add, ["ang_t"], ["kk_t"])
        ts("dve", kk_t[RW, :], kk_t[RW, :], M_RND, None, ALU.subtract, None, ["kk_t"], ["kk_t"])
        stt(ang_t[RW, :], kk_t[RW, :], -C1, ang_t[RW, :], ALU.mult, ALU.add, ["kk_t", "ang_t"], ["ang_t"])
        stt(ang_t[RW, :], kk_t[RW, :], -C2, ang_t[RW, :], ALU.mult, ALU.add, ["kk_t", "ang_t"], ["ang_t"])
        ts("dve", rc_t[RW, :], ang_t[RW, :], math.pi / 2, None, ALU.add, None, ["ang_t"], ["rc_t"])
        ts("dve", kk_t[RW, :], rc_t[RW, :], math.pi, -2 * math.pi, ALU.is_gt, ALU.mult, ["rc_t"], ["kk_t"])
        tt("dve", rc_t[RW, :], rc_t[RW, :], kk_t[RW, :], ALU.add, ["rc_t", "kk_t"], ["rc_t"])
        ts("dve", rc_t[RW, :], rc_t[RW, :], PI_LO, -PI_LO, ALU.min, ALU.max, ["rc_t"], ["rc_t"])
        act(rc_t[RW, :], rc_t[RW, :], AF.Sin, ["rc_t"], ["rc_t"])
        dma(ropeC_d[:, cols], rc_t[RW, :], ["rc_t"], [("ropeC", c8)])
        ts("dve", ang_t[RW, :], ang_t[RW, :], PI_LO, -PI_LO, ALU.min, ALU.max, ["ang_t"], ["ang_t"])
        act(ang_t[RW, :], ang_t[RW, :], AF.Sin, ["ang_t"], ["ang_t"])
        ts("dve", ang_t[RW, :], ang_t[RW, :], pcols[RW, 98:99], None, ALU.mult, None, ["ang_t", "pcols"], ["ang_t"])
        dma(ropeS_d[:, cols], ang_t[RW, :], ["ang_t"], [("ropeS", c8)])

    CH1 = 256
    W = Carver(P1_persist_end + 4 * 2048 + 128)
    xt = [W.take([128, D]) for _ in range(2)]
    xn = W.take([128, D])
    h0T = [W.take([128, 8, CH1], BF16) for _ in range(2)]
    ub = [W.take([128, CH1 + 3]) for _ in range(2)]
    hal = W.take([128, 8, 4])
    acc = [W.take([128, CH1]) for _ in range(2)]
    th = [W.take([128, CH1]) for _ in range(2)]
    xsT = [W.take([128, 4, CH1]) for _ in range(2)]
    BCT = [W.take([128, 4, CH1], BF16) for _ in range(2)]
    lat = W.take([128, 5, CH1])
    sq = [W.take([128, CH1]) for _ in range(2)]
    rsb = [W.take([128, CH1]) for _ in range(2)]
    rcs = W.take([128, 2, CH1])
    rtmp = W.take([128, 2, CH1])
    zs = W.take([128, 512])
    dts2 = [W.take([128, 8, 8]) for _ in range(2)]
    Wt = W.take([128, 8, 128])
    Urow = W.take([128, 128])
    apad = W.take([128, 64])
    NEG4b = W.take([128, 512], BF16)
    NEGSEL = W.take([128, 8])
    Dm2 = [W.take([128, 8, 128]) for _ in range(2)]
    MT = W.take([128, 8, 128], BF16)
    Btok = W.take([128, 2, 128], BF16)
    xdt = W.take([128, 512], BF16)
    xdtd = W.take([128, 512], BF16)
    Sst = W.take([128, 512])
    Sbf = W.take([128, 512], BF16)
    y1 = W.take([128, 512])
    y2 = W.take([128, 512])
    ssq = W.take([128, 8])
    P1_END = W.off
    assert P1_END <= ARENA_BYTES, P1_END

    memset("pool", hal, 0.0, ["hal"])
    memset("pool", Sst, 0.0, ["Sst"])
    memset("pool", Sbf, 0.0, ["Sbf"])
    memset("pool", Wt, 0.0, ["Wt"])
    memset("pool", Urow, 0.0, ["Urow"])
    memset("pool", apad, 0.0, ["apad"])
    memset("pool", NEGSEL, 0.0, ["NEGSEL"])
    memset("dve", Urow[32:40, :], 1.0, ["Urow"])
    cp("dve", Wt[0:8, :, :], ident[0:8, 0:8].unsqueeze(2).broadcast_to([8, 8, 128]), ["cst", "Wt"], ["Wt"])
    ts("dve", NEGSEL[32:40, :], ident[0:8, 0:8], -1.0, None, ALU.mult, None, ["cst", "NEGSEL"], ["NEGSEL"])
    for q4 in range(4):
        ts("dve", NEG4b[:, q4 * 128:(q4 + 1) * 128], tri, -1.0, 30000.0, ALU.add, ALU.mult, ["cst"], ["NEG4b"])

    actr = [0]
    bctr = [0]
    sqc = [0]

    def pbA():
        b = actr[0] % 2
        actr[0] += 1
        return b

    def pbB():
        b = 5 + bctr[0] % 3
        bctr[0] += 1
        return b

    def load_x(t):
        dma(xt[t % 2], x_d[t * 128:(t + 1) * 128, :], [], [("xt", t % 2)])

    def ln0_tile(t, j, par):
        xk = ("xt", t % 2)
        rstd, nmr, sck = ln_stats([xt[t % 2][:, 0:512], xt[t % 2][:, 512:1024]], [xk])
        act(xn, xt[t % 2], AF.Identity, [xk, sck], ["xn"], scale=rstd, bias=nmr)
        if t + 2 < NT:
            load_x(t + 2)
        for half in range(2):
            b = pbA()
            for q in range(4):
                ft = half * 4 + q
                tp(pbank[b][:, q * 128:(q + 1) * 128], xn[:, ft * 128:(ft + 1) * 128], ident, ["xn", "cst"], [pbk(b)])
            for q in range(4):
                ft = half * 4 + q
                act(h0T[par][:, ft, j * 128:(j + 1) * 128], pbank[b][:, q * 128:(q + 1) * 128], AF.Identity,
                    [pbk(b), "pcols"], [("h0T", par)], scale=pcols[:, ft:ft + 1], bias=pcols[:, 8 + ft:9 + ft])

    def proj_fm(par, col0, M):
        b = pbA()
        for kt in range(8):
            mm(pbank[b][0:M, 0:CH1], w_in[:, kt, col0:col0 + M], h0T[par][:, kt, :], kt == 0, kt == 7,
               [wkey(col0), ("h0T", par)], [pbk(b)])
        return b

    def stage_A(c):
        par = c % 2
        gcols = slice(c * CH1, (c + 1) * CH1)
        if c >= 1:
            issue_casts(2)
        for j in range(2):
            ln0_tile(2 * c + j, j, par)
        s.mark("sig", ("h0T", c))
        c8 = (c * CH1) // 512
        dma(rcs[RW, 0, :], ropeC_d[:, gcols], [("ropeC", c8)], ["rcs"])
        dma(rcs[RW, 1, :], ropeS_d[:, gcols], [("ropeS", c8)], ["rcs"])
        for ct in range(8):
            b = proj_fm(par, 512 + ct * 128, 128)
            u = ub[ct % 2]
            uk = ("ub", ct % 2)
            a = acc[ct % 2]
            ak = ("acc", ct % 2)
            tht = th[ct % 2]
            thk = ("th", ct % 2)
            cp("pool", u[:, 0:3], hal[:, ct, 0:3], ["hal"], [uk])
            act(u[:, 3:3 + CH1], pbank[b][:, 0:CH1], AF.Copy, [pbk(b)], [uk])
            act(a, pbank[b][:, 0:CH1], AF.Identity, [pbk(b), "pcols"], [ak],
                scale=pcols[:, 48 + ct * 4 + 3:48 + ct * 4 + 4], bias=pcols[:, 80 + ct:81 + ct])
            for k in range(3):
                stt(a, u[:, k:k + CH1], pcols[:, 48 + ct * 4 + k:48 + ct * 4 + k + 1], a, ALU.mult, ALU.add,
                    [uk, ak, "pcols"], [ak])
            cp("pool", hal[:, ct, 0:3], u[:, CH1:CH1 + 3], [uk], ["hal"])
            act(tht, a, AF.Tanh, [ak], [thk], scale=0.5)
            ts("dve", tht, tht, 1.0, 0.5, ALU.add, ALU.mult, [thk], [thk])
            if ct < 4:
                tt("pool", xsT[par][:, ct, :], tht, a, ALU.mult, [thk, ak], [("xsT", par)])
            else:
                tt("pool", BCT[par][:, ct - 4, :], tht, a, ALU.mult, [thk, ak], [("BCT", par)])
        for (i0, n, gcol, dstT, nm) in ((0, 3, 92, qnT, "qnT"), (3, 2, 95, kvnT, "kvnT")):
            for i in range(n):
                b = proj_fm(par, 1544 + (i0 + i) * 128, 128)
                act(lat[:, i0 + i, :], pbank[b][:, 0:CH1], AF.Copy, [pbk(b)], [("lat", i0 + i)])
                si = sqc[0] % 2
                sqc[0] += 1
                act(sq[si], pbank[b][:, 0:CH1], AF.Square, [pbk(b)], [("sq", si)])
                mm(pbank[2][:, 0:CH1], ones, sq[si], i == 0, i == n - 1, ["cst", ("sq", si)], [pbk(2)])
            r = rsb[0 if i0 == 0 else 1]
            rk = ("rsb", i0)
            act(r, pbank[2][:, 0:CH1], AF.Ln, [pbk(2)], [rk], scale=1.0 / (128 * n), bias=RMS_EPS)
            act(r, r, AF.Exp, [rk], [rk], scale=-0.5)
            for i in range(n):
                stt(dstT[:, i, gcols], lat[:, i0 + i, :], pcols[:, gcol + i:gcol + i + 1], r, ALU.mult, ALU.mult,
                    [("lat", i0 + i), rk, "pcols"], [(nm, c)])
        b1 = proj_fm(par, 2120, 96)
        b2 = proj_fm(par, 2216, 96)
        tt("dve", rtmp[RW, 0, :], pbank[b1][RW, 0:CH1], rcs[RW, 0, :], ALU.mult, [pbk(b1), "rcs"], ["rtmp0"])
        tt("dve", rtmp[RW, 1, :], pbank[b2][RW, 0:CH1], rcs[RW, 1, :], ALU.mult, [pbk(b2), "rcs"], ["rtmp1"])
        tt("dve", KT0[RW, gcols], rtmp[RW, 0, :], rtmp[RW, 1, :], ALU.add, ["rtmp0", "rtmp1"], [("KT0r", c)])

    def part_I(t):
        c, j = t // 2, t % 2
        par = c % 2
        tp_ = t % 2
        dts, Dm = dts2[tp_], Dm2[tp_]
        dk = lambda nm: (nm, tp_)
        hk = ("h0T", par)
        H0 = h0T[par]
        tcols = slice(j * 128, (j + 1) * 128)
        s.mark("await", ("h0T", c))
        bs = 3
        for kt in range(8):
            mm(pbank[bs][:, 0:8], H0[:, kt, tcols], w_in[:, kt, 1536:1544], kt == 0, kt == 7, [hk, ("w_in", 3)], [pbk(bs)])
        tt("dve", dts[:, 0, :], pbank[bs][:, 0:8], prow[:, 0:8], ALU.add, [pbk(bs), "prow"], [dk("dts0")])
        act(dts[:, 0, :], dts[:, 0, :], AF.Exp, [dk("dts0")], [dk("dts0")])
        act(dts[:, 1, :], dts[:, 0, :], AF.Ln, [dk("dts0")], [dk("dt")], bias=1.0, scale=1.0)
        tt("dve", dts[:, 2, :], dts[:, 1, :], Abc, ALU.mult, [dk("dt"), "Abc"], [dk("a")])
        mm(pbank[bs][:, 8:16], tri, dts[:, 2, :], True, True, ["cst", dk("a")], [pbk(bs)])
        cp("dve", dts[:, 3, :], pbank[bs][:, 8:16], [pbk(bs)], [dk("acs")])
        act(dts[:, 4, :], dts[:, 3, :], AF.Exp, [dk("acs")], [dk("eacs")])
        mm(pbank[bs][:, 16:24], ones, dts[:, 2, :], True, True, ["cst", dk("a")], [pbk(bs)])
        tt("dve", dts[:, 5, :], pbank[bs][:, 16:24], dts[:, 3, :], ALU.subtract, [pbk(bs), dk("acs")], [dk("dte")])
        act(dts[:, 7, :], pbank[bs][:, 16:24], AF.Exp, [pbk(bs)], [dk("cd")])
        cp("dve", apad.rearrange("p (r c) -> p r c", c=32)[:, :, 0:8], dts[:, 2, :].unsqueeze(1).broadcast_to([128, 2, 8]),
           [dk("a"), "apad"], ["apad"])
        bq = 4
        mm(pbank[bq][0:40, 0:128], apad[:, 0:40], tri, True, True, ["apad", "cst"], [pbk(bq)])
        cp("dve", Urow[0:8, :], pbank[bq][0:8, 0:128], [pbk(bq), "Urow"], ["Urow"])
        tt("dve", Wt[32:40, :, :], pbank[bq][32:40, 0:128].unsqueeze(1).broadcast_to([8, 8, 128]),
           NEGSEL[32:40, :].unsqueeze(2).broadcast_to([8, 8, 128]), ALU.mult, [pbk(bq), "NEGSEL", "Wt"], ["Wt"])
        for hh in range(2):
            bd_ = 4 if hh == 0 else 3
            mm(pbank[bd_], identb, NEG4b, True, False, ["identb", "NEG4b"], [pbk(bd_)])
            for h in range(4 * hh, 4 * hh + 4):
                mm(pbank[bd_][:, (h % 4) * 128:(h % 4 + 1) * 128], Wt[0:40, h, :], Urow[0:40, :], False, h % 4 == 3,
                   ["Wt", "Urow"], [pbk(bd_)])
            act(Dm[:, 4 * hh:4 * hh + 4, :].rearrange("p a b -> p (a b)"), pbank[bd_], AF.Exp, [pbk(bd_)], [dk("Dm")])
        act(dts[:, 5, :], dts[:, 5, :], AF.Exp, [dk("dte")], [dk("dte")])
        tt("dve", dts[:, 6, :], dts[:, 1, :], dts[:, 5, :], ALU.mult, [dk("dt"), dk("dte")], [dk("wdt")])
        s.mark("sig", ("PI", t))

    def part_II(t):
        c, j = t // 2, t % 2
        par = c % 2
        tp_ = t % 2
        dts, Dm = dts2[tp_], Dm2[tp_]
        dk = lambda nm: (nm, tp_)
        hk, xk_, bk_ = ("h0T", par), ("xsT", par), ("BCT", par)
        H0, XS, BC = h0T[par], xsT[par], BCT[par]
        tcols = slice(j * 128, (j + 1) * 128)
        gt = slice(t * 128, (t + 1) * 128)
        s.mark("await", ("PI", t))
        if True:
            bz = pbB()
            for kt in range(8):
                mm(pbank[bz], H0[:, kt, tcols], w_in[:, kt, 0:512], kt == 0, kt == 7, [hk, ("w_in", 0)], [pbk(bz)])
            act(zs, pbank[bz], AF.Tanh, [pbk(bz)], ["zs"], scale=0.5)
            ts("dve", zs, zs, 1.0, 0.5, ALU.add, ALU.mult, ["zs"], ["zs"])
            tt("dve", zs, zs, pbank[bz], ALU.mult, ["zs", pbk(bz)], ["zs"])
            bx = pbB()
            for ct in range(4):
                tp(pbank[bx][:, ct * 128:(ct + 1) * 128], XS[:, ct, tcols], ident, [xk_, "cst"], [pbk(bx)])
            xs3 = pbank[bx].rearrange("p (h q) -> p h q", h=8)
            tt("dve", xdt.rearrange("p (h q) -> p h q", h=8), xs3, dts[:, 1, :].unsqueeze(2).broadcast_to([128, 8, 64]),
               ALU.mult, [pbk(bx), dk("dt")], ["xdt"])
            tt("dve", xdtd.rearrange("p (h q) -> p h q", h=8), xs3, dts[:, 6, :].unsqueeze(2).broadcast_to([128, 8, 64]),
               ALU.mult, [pbk(bx), dk("wdt")], ["xdtd"])
            tt("dve", y2.rearrange("p (h q) -> p h q", h=8), xs3, prow[:, 16:24].unsqueeze(2).broadcast_to([128, 8, 64]),
               ALU.mult, [pbk(bx), "prow"], ["y2"])
            bb = pbB()
            pbb = pbank[bb].bitcast(BF16)
            for g in range(2):
                tp(pbb[:, g * 128:(g + 1) * 128], BC[:, g, tcols], identb, [bk_, "identb"], [pbk(bb)])
            cp("dve", Btok.rearrange("p a b -> p (a b)"), pbb[:, 0:256], [pbk(bb)], ["Btok"])
            bc = pbB()
            for g in range(2):
                mm(pbank[bc][:, g * 128:(g + 1) * 128], BC[:, g, tcols], BC[:, 2 + g, tcols], True, True, [bk_], [pbk(bc)])
            for g in range(2):
                tt("dve", MT[:, 4 * g:4 * g + 4, :], Dm[:, 4 * g:4 * g + 4, :],
                   pbank[bc][:, g * 128:(g + 1) * 128].unsqueeze(1).broadcast_to([128, 4, 128]), ALU.mult,
                   [dk("Dm"), pbk(bc)], ["MT"])
            by = pbB()
            for h in range(8):
                mm(pbank[by][:, h * 64:(h + 1) * 64], MT[:, h, :], xdt[:, h * 64:(h + 1) * 64], True, True, ["MT", "xdt"], [pbk(by)])
            bo = pbB()
            for g in range(2):
                mm(pbank[bo][:, g * 256:(g + 1) * 256], BC[:, 2 + g, tcols], Sbf[:, g * 256:(g + 1) * 256], True, True,
                   [bk_, "Sbf"], [pbk(bo)])
            bd = pbB()
            for g in range(2):
                mm(pbank[bd][:, g * 256:(g + 1) * 256], Btok[:, g, :], xdtd[:, g * 256:(g + 1) * 256], True, True,
                   ["Btok", "xdtd"], [pbk(bd)])
            tt("dve", y1.rearrange("p (h q) -> p h q", h=8), pbank[bo].rearrange("p (h q) -> p h q", h=8),
               dts[:, 4, :].unsqueeze(2).broadcast_to([128, 8, 64]), ALU.mult, [pbk(bo), dk("eacs")], ["y1"])
            tt("dve", y1, y1, pbank[by], ALU.add, ["y1", pbk(by)], ["y1"])
            tt("pool", y1, y1, y2, ALU.add, ["y1", "y2"], ["y1"])
            tt("pool", y1, y1, zs, ALU.mult, ["y1", "zs"], ["y1"])
            tt("pool", Sst.rearrange("p (h q) -> p h q", h=8), Sst.rearrange("p (h q) -> p h q", h=8),
               dts[:, 7, :].unsqueeze(2).broadcast_to([128, 8, 64]), ALU.mult, ["Sst", dk("cd")], ["Sst"])
            tt("dve", Sst, Sst, pbank[bd], ALU.add, ["Sst", pbk(bd)], ["Sst"])
            cp("pool", Sbf, Sst, ["Sst"], ["Sbf"])
            for g in range(2):
                act(y2[:, g * 256:(g + 1) * 256], y1[:, g * 256:(g + 1) * 256], AF.Square, ["y1", "y2"], ["y2", "ssq"],
                    accum_out=ssq[:, g:g + 1])
            ts("pool", ssq[:, 2:4], ssq[:, 0:2], 1.0 / 256, RMS_EPS, ALU.mult, ALU.add, ["ssq"], ["ssq"])
            tt("pool", ssq[:, 4:6], ssq[:, 2:4], nhalf[:, 0:2], ALU.pow, ["ssq", "nhalf"], ["ssq"])
            for g in range(2):
                ts("dve", y2[:, g * 256:(g + 1) * 256], y1[:, g * 256:(g + 1) * 256], ssq[:, 4 + g:5 + g], None, ALU.mult, None,
                   ["y1", "ssq", "y2"], ["y2"])
            bt = pbB()
            for ct in range(4):
                tp(pbank[bt][:, ct * 128:(ct + 1) * 128], y2[:, ct * 128:(ct + 1) * 128], ident, ["y2", "cst"], [pbk(bt)])
            for ct in range(4):
                act(yT[:, ct, gt], pbank[bt][:, ct * 128:(ct + 1) * 128], AF.Identity, [pbk(bt), "pcols"],
                    [("yT", t)], scale=pcols[:, 88 + ct:89 + ct], bias=0.0)

    load_x(0)
    load_x(1)
    NCH1 = S // CH1
    stage_A(0)
    part_I(0)
    for c in range(NCH1):
        t0, t1 = 2 * c, 2 * c + 1
        sx = s.capture(lambda: stage_A(c + 1)) if c + 1 < NCH1 else []
        sy = s.capture(lambda: (part_II(t0), part_II(t1)))
        sz = s.capture(lambda: (part_I(t1), part_I(t0 + 2) if t0 + 2 < NT else None))
        s.merge([sx, sy, sz])

    issue_casts(NBLK)
    s.barrier()
    if debug:
        for nm, src, shp in (("yT", yT, [128, 4, S]), ("qnT", qnT, [128, 3, S]), ("kvnT", kvnT, [128, 2, S]), ("KT0", KT0, [128, S])):
            d = dbg_out(nm, shp, BF16)
            dma(d, src, [], [("dbg", nm)])
        s.barrier()
    if stop_after <= 1:
        s.emit()
        return nc, dbg

    P2 = Carver(P1_persist_end)
    KT1 = P2.take([128, S], BF16)
    wq2 = P2.take([128, 8, 3, 2 * 96], BF16)
    wkv2 = P2.take([128, 2, 1024], BF16)
    ropeC = P2.take([128, S])
    ropeS = P2.take([128, S])
    QT = [P2.take([128, 512], BF16) for _ in range(3)]
    PT = [P2.take([128, 512], BF16) for _ in range(6)]
    Vev = P2.take([128, 32, 72], BF16)
    Vod = P2.take([128, 32, 128], BF16)
    rt1 = P2.take([128, 512])
    rt2 = P2.take([128, 512])
    rcp = P2.take([128, 512])
    bcs = P2.take([128, 512])
    KT = [KT0, KT1]

    dma(wq2.rearrange("p a b c -> p (a b c)"), wq2_d, [], ["wq2"], eng="pool")
    dma(wkv2.rearrange("p a b -> p (a b)"), wkv2_d, [], ["wkv2"], eng="pool")
    dma(ropeC[RW, :], ropeC_d, [], ["ropeC"])
    dma(ropeS[RW, :], ropeS_d, [], ["ropeS"])
    cp("pool", KT1[RW, :], KT0[RW, :], [], [("KT", 1)])
    memset("dve", Vev[:, :, 64:65], 1.0, ["Vev"])
    memset("pool", Vod, 0.0, ["Vod"])
    memset("pool", Vod[:, :, 0:1], 1.0, ["Vod"])

    PS_S = [0, 1, 2, 3]
    PS_O = [4, 5]
    PS_X = [6, 7]
    sctr = [0]
    qctr = [0]
    pctr = [0]
    octr = [0]
    xctr = [0]
    rtc = [0]
    import collections
    bg = collections.deque()

    def xbank():
        b = PS_X[xctr[0] % 2]
        xctr[0] += 1
        return b

    def bg_pop(n=1):
        for _ in range(n):
            if bg:
                bg.popleft()()

    def head_params(h):
        par = h % 2
        return dict(par=par, KTh=KT[par], ktk=("KT", par), V=(Vev if par == 0 else Vod), vk=("Vev" if par == 0 else "Vod"),
                    voff=(0 if par == 0 else 64), Mv=(65 if par == 0 else 128),
                    orow=(slice(0, 64) if par == 0 else slice(64, 128)), drow=(64 if par == 0 else 0))

    def kv_units(h):
        hp = head_params(h)
        units = []
        for c8 in range(8):
            def u(c8=c8):
                cols = slice(c8 * 512, (c8 + 1) * 512)
                b = xbank()
                for kt in range(2):
                    mm(pbank[b][0:64, :], wkv2[:, kt, h * 128:h * 128 + 64], kvnT[:, kt, cols], kt == 0, kt == 1, ["wkv2"], [pbk(b)])
                cp("dve", hp["KTh"][0:64, cols], pbank[b][0:64, :], [pbk(b)], [hp["ktk"]])
            units.append(u)
        for b4 in range(4):
            def u(b4=b4):
                b = xbank()
                for bl in range(8):
                    blk = b4 * 8 + bl
                    for kt in range(2):
                        mm(pbank[b][:, bl * 64:(bl + 1) * 64], kvnT[:, kt, blk * 128:(blk + 1) * 128],
                           wkv2[:, kt, h * 128 + 64:h * 128 + 128], kt == 0, kt == 1, ["wkv2"], [pbk(b)])
                cp("dve", hp["V"][:, b4 * 8:(b4 + 1) * 8, hp["voff"]:hp["voff"] + 64], pbank[b].rearrange("p (a b) -> p a b", a=8),
                   [pbk(b)], [hp["vk"]])
            units.append(u)
        return units

    qslots = {}

    def q_unit(h, j):
        def u():
            qcols = slice(j * 512, (j + 1) * 512)
            ba = xbank()
            for kt in range(3):
                mm(pbank[ba][0:96, :], wq2[:, h, kt, 0:96], qnT[:, kt, qcols], kt == 0, kt == 2, ["wq2"], [pbk(ba)])
            qi = qctr[0] % 3
            qctr[0] += 1
            Q = QT[qi]
            qk = ("QT", qi)
            qslots[(h, j)] = (Q, qk)
            ts("dve", Q[0:64, :], pbank[ba][0:64, :], SC_MLA, None, ALU.mult, None, [pbk(ba)], [qk])
            stt(rt1[RW, :], pbank[ba][RW, :], SC_MLA, ropeC[RW, qcols], ALU.mult, ALU.mult, [pbk(ba), "ropeC"], ["rt1"])
            s.op("dve", lambda e, ba=ba: e.stream_shuffle(out=rt2[RW, :], in_=pbank[ba][RW, :],
                                                           mask=list(range(16, 32)) + list(range(0, 16))),
                 [pbk(ba)], ["rt2"], 650)
            stt(rt2[RW, :], rt2[RW, :], SC_MLA, ropeS[RW, qcols], ALU.mult, ALU.mult, ["rt2", "ropeS"], ["rt2"])
            tt("pool", Q[RW, :], rt1[RW, :], rt2[RW, :], ALU.add, ["rt1", "rt2"], [qk])
        return u

    def norm_unit(h, j, ob):
        hp = head_params(h)

        def u():
            qcols = slice(j * 512, (j + 1) * 512)
            drow, orow = hp["drow"], hp["orow"]
            o0 = orow.start
            for qd in range(2):
                s.op("dve", lambda e, qd=qd: e.stream_shuffle(out=bcs[o0 + 32 * qd:o0 + 32 * qd + 32, :],
                                                               in_=rcp[drow:drow + 32, :], mask=[0] * 32),
                     ["rcp"], ["bcs"], 650)
            tt("dve", oT[orow, h // 2, qcols], pbank[ob][orow, :], bcs[orow, :], ALU.mult, [pbk(ob), "bcs"], [("oT", h, j)])
        return u

    for u in kv_units(0):
        u()
    q_unit(0, 0)()
    order = [(h, j) for h in range(8) for j in range(8)]
    for idx, (h, j) in enumerate(order):
        hp = head_params(h)
        KTh, ktk, V, vk, Mv = hp["KTh"], hp["ktk"], hp["V"], hp["vk"], hp["Mv"]
        Q, qk = qslots[(h, j)]
        if idx + 1 < len(order):
            bg.append(q_unit(*order[idx + 1]))
        if h + 1 < 8 and j >= 5:
            ku = kv_units(h + 1)
            share = {5: ku[0:3], 6: ku[3:7], 7: ku[7:12]}[j]
            bg.extend(share)
        ob = PS_O[octr[0] % 2]
        octr[0] += 1
        nkb = 4 * j + 4
        sbanks = {}
        pslots = {}

        def emit_S(kb):
            n0 = max(0, kb - 4 * j) * 128
            sb_ = PS_S[sctr[0] % 4]
            sctr[0] += 1
            sbanks[kb] = sb_
            mm(pbank[sb_][:, n0:512], KTh[0:96, kb * 128:(kb + 1) * 128], Q[0:96, n0:512], True, True, [ktk, qk], [pbk(sb_)])

        emit_S(0)
        if nkb > 1:
            emit_S(1)
        for kb in range(nkb):
            n0 = max(0, kb - 4 * j) * 128
            sb_ = sbanks[kb]
            pi = pctr[0] % 6
            pctr[0] += 1
            P = PT[pi]
            pk = ("PT", pi)
            act(P[:, n0:512], pbank[sb_][:, n0:512], AF.Exp, [pbk(sb_)], [pk])
            if kb >= 4 * j:
                tt("pool", P[:, n0:n0 + 128], P[:, n0:n0 + 128], trib, ALU.mult, [pk, "trib"], [pk])
            if kb + 2 < nkb:
                emit_S(kb + 2)
            mm(pbank[ob][0:Mv, n0:512], V[:, kb, 0:Mv], P[:, n0:512], kb == 0, kb == nkb - 1, [vk, pk], [pbk(ob)])
            if kb >= 1:
                bg_pop(1)
        drow = hp["drow"]
        s.op("dve", lambda e, ob=ob, drow=drow: e.reciprocal(out=rcp[drow:drow + 1, :], in_=pbank[ob][drow:drow + 1, :]),
             [pbk(ob)], ["rcp"], 3400)
        bg_pop(len(bg))
        bg.append(norm_unit(h, j, ob))
    bg_pop(len(bg))

    s.barrier()
    if debug:
        d = dbg_out("oT", [128, 4, S], BF16)
        dma(d, oT, [], [("dbg", "oT")])
        s.barrier()
    if stop_after <= 2:
        s.emit()
        return nc, dbg

    P3 = Carver(R_after_oT)
    kmemT = P3.take([128, 8, 256], BF16)
    vmem = P3.take([128, 2, 1024], BF16)
    wr = [P3.take([128, 4096], BF16) for _ in range(3)]
    tb = [P3.take([128, D]) for _ in range(6)]
    tbc = [0]

    def take4():
        ids = [(tbc[0] + k) % 6 for k in range(4)]
        tbc[0] += 4
        return ids
    rn = tb
    hres = P3.take([128, 8, 512])
    bfA = P3.take([128, 8, 512], BF16)
    bfB = P3.take([128, 8, 512], BF16)
    aT = P3.take([128, 32, 512], BF16)
    PTm2 = [P3.take([128, 2, 512], BF16) for _ in range(2)]
    rden = P3.take([128, 512])
    sqf = [P3.take([128, 512]) for _ in range(2)]
    g3b = P3.take([128, 2, D])

    dma(g3b.rearrange("p a b -> p (a b)"), prows_d[:, 24:24 + 2048].partition_broadcast(128), [], ["g3b"])

    seq = [22, 23, 24, 25]
    for c in range(8):
        seq += list(range(0, 22))
    wpos = [0]
    wissued = [0]

    def w_issue():
        i = wissued[0]
        if i < len(seq):
            dma(wr[i % 3], wbf_d[seq[i]], [], [("wr", i % 3)])
            wissued[0] += 1

    def w_next():
        i = wpos[0]
        wpos[0] += 1
        while wissued[0] < min(len(seq), i + 3):
            w_issue()
        return wr[i % 3], ("wr", i % 3)

    w_issue()
    w_issue()

    memT = bfA
    for mb in range(2):
        dma(rn[mb], mem_d[mb * 128:(mb + 1) * 128, :], [], [("rn", mb)])
    for mb in range(2):
        for half in range(2):
            b = s.pb()
            for q in range(4):
                ft = half * 4 + q
                tp(pbank[b][:, q * 128:(q + 1) * 128], rn[mb][:, ft * 128:(ft + 1) * 128], ident, [("rn", mb)], [pbk(b)])
            for q in range(4):
                ft = half * 4 + q
                act(memT[:, ft, mb * 128:(mb + 1) * 128], pbank[b][:, q * 128:(q + 1) * 128], AF.Copy, [pbk(b)], ["bfA"])
    for blk in range(2):
        wk, wkk = w_next()
        wk3 = wk.rearrange("p (a b) -> p a b", a=8)
        for q in range(4):
            ft = blk * 4 + q
            b = s.pb()
            for kt in range(8):
                mm(pbank[b][:, 0:256], wk3[:, kt, q * 128:(q + 1) * 128], memT[:, kt, 0:256], kt == 0, kt == 7, [wkk, "bfA"], [pbk(b)])
            act(kmemT[:, ft, :], pbank[b][:, 0:256], AF.Copy, [pbk(b)], ["kmemT"])
    for blk in range(2):
        wv, wvk = w_next()
        wv3 = wv.rearrange("p (a b) -> p a b", a=8)
        for mb in range(2):
            b = s.pb()
            for kt in range(8):
                mm(pbank[b], memT[:, kt, mb * 128:(mb + 1) * 128], wv3[:, kt, :], kt == 0, kt == 7, [wvk, "bfA"], [pbk(b)])
            act(vmem[:, mb, blk * 512:(blk + 1) * 512], pbank[b], AF.Copy, [pbk(b)], ["vmem"])

    HK = [("hres", j) for j in range(4)]
    RK = [("rn", j) for j in range(4)]

    def tok_to_fm4(ids, gcol, bcol, agcol, abcol, dstb):
        for ft in range(8):
            b = s.pb()
            for j in range(4):
                tp(pbank[b][:, j * 128:(j + 1) * 128], tb[ids[j]][:, ft * 128:(ft + 1) * 128], ident, [("rn", ids[j]), "cst"], [pbk(b)])
            if dstb is not None:
                act(dstb[0][:, ft, :], pbank[b], AF.Identity, [pbk(b), "pcols"], [dstb[1]],
                    scale=pcols[:, gcol + ft:gcol + ft + 1], bias=pcols[:, bcol + ft:bcol + ft + 1])
            ts("dve", hres[:, ft, :], pbank[b], agab[:, agcol + ft:agcol + ft + 1], agab[:, abcol + ft:abcol + ft + 1],
               ALU.mult, ALU.add, [pbk(b), "agab"], HK)

    def ln_stage(gcol, bcol, agcol, abcol, dstb, final_c=None):
        bks = []
        for j in range(4):
            bj = []
            for half in range(2):
                b = s.pb()
                bj.append(b)
                for q in range(4):
                    ft = half * 4 + q
                    tp(pbank[b][:, q * 128:(q + 1) * 128], hres[:, ft, j * 128:(j + 1) * 128], ident, [("hres", j), "cst"], [pbk(b)])
            bks.append(bj)
        idx = [ln_stats_a([pbank[bks[j][0]], pbank[bks[j][1]]], [pbk(bks[j][0]), pbk(bks[j][1])]) for j in range(4)]
        for j in range(4):
            ln_stats_b(idx[j])
        st = [ln_stats_c(idx[j]) for j in range(4)]
        ids = take4()
        for j in range(4):
            r_, rk_ = tb[ids[j]], ("rn", ids[j])
            act(r_[:, 0:512], pbank[bks[j][0]], AF.Identity, [pbk(bks[j][0]), st[j][2]], [rk_],
                scale=st[j][0], bias=st[j][1])
            ts("dve", r_[:, 512:1024], pbank[bks[j][1]], st[j][0], st[j][1], ALU.mult, ALU.add,
               [pbk(bks[j][1]), st[j][2]], [rk_])
        if final_c is None:
            tok_to_fm4(ids, gcol, bcol, agcol, abcol, dstb)
        else:
            for j in range(4):
                t = final_c * 4 + j
                r_, rk_ = tb[ids[j]], ("rn", ids[j])
                tt("pool", r_, r_, g3b[:, 0, :], ALU.mult, [rk_, "g3b"], [rk_])
                tt("dve", r_, r_, g3b[:, 1, :], ALU.add, [rk_, "g3b"], [rk_])
                dma(out_d[t * 128:(t + 1) * 128, :], r_, [rk_], [("out", t)], eng="pool")

    def branch_fm(nblk_cols, rhs_fn, rkeys, ktn):
        for blk in range(2):
            wb, wbk = w_next()
            wb3 = wb.rearrange("p (a b) -> p a b", a=8)
            for q in range(4):
                ft = blk * 4 + q
                b = s.pb()
                for kt in range(ktn):
                    mm(pbank[b], wb3[:, kt, q * 128:(q + 1) * 128], rhs_fn(kt), kt == 0, kt == ktn - 1, [wbk] + rkeys, [pbk(b)])
                yield ft, b

    rctr = [0]
    for c in range(p3_chunks if p3_stage >= 0 else 0):
        ccols = slice(c * 512, (c + 1) * 512)
        ids = take4()
        for j in range(4):
            t = c * 4 + j
            r_, rk_ = tb[ids[j]], ("rn", ids[j])
            dma(r_, x_d[t * 128:(t + 1) * 128, :], [], [rk_])
            i = ln_stats_a([r_[:, 0:512], r_[:, 512:1024]], [rk_])
            ln_stats_b(i)
            rstd, nmr, sck = ln_stats_c(i)
            act(r_, r_, AF.Identity, [rk_, sck], [rk_], scale=rstd, bias=nmr)
        tok_to_fm4(ids, 0, 8, 0, 8, None)
        if p3_stage < 1:
            continue
        for ft, b in branch_fm(None, lambda kt: (yT if kt < 4 else oT)[:, kt % 4, ccols], [], 8):
            tt("dve", hres[:, ft, :], hres[:, ft, :], pbank[b], ALU.add, HK + [pbk(b)], HK)
        if p3_stage < 2:
            continue
        ln_stage(16, 24, 16, 24, (bfA, "bfA"))
        if p3_stage < 3:
            continue
        for blk in range(2):
            wb, wbk = w_next()
            wb3 = wb.rearrange("p (a b) -> p a b", a=8)
            for q in range(4):
                ft = blk * 4 + q
                b = s.pb()
                for kt in range(8):
                    mm(pbank[b], wb3[:, kt, q * 128:(q + 1) * 128], bfA[:, kt, :], kt == 0, kt == 7, [wbk, "bfA"], [pbk(b)])
                act(bfB[:, ft, :], pbank[b], AF.Copy, [pbk(b)], [("bfB", ft)], scale=SC_MEM)
        for mh in range(4):
            PTm = PTm2[mh % 2]
            pk_ = lambda mb, mh=mh: ("PTm", mh % 2, mb)
            bden = s.pb()
            for mb in range(2):
                b = s.pb()
                for i in range(2):
                    mm(pbank[b], kmemT[:, 2 * mh + i, mb * 128:(mb + 1) * 128], bfB[:, 2 * mh + i, :], i == 0, i == 1,
                       ["kmemT", ("bfB", 2 * mh + i)], [pbk(b)])
                act(PTm[:, mb, :], pbank[b], AF.Exp, [pbk(b)], [pk_(mb)])
            for mb in range(2):
                mm(pbank[bden], onesb, PTm[:, mb, :], mb == 0, mb == 1, ["onesb", pk_(mb)], [pbk(bden)])
            act(rden, pbank[bden], AF.Ln, [pbk(bden)], ["rden"])
            act(rden, rden, AF.Exp, ["rden"], ["rden"], scale=-1.0)
            for i in range(2):
                b = s.pb()
                for mb in range(2):
                    mm(pbank[b], vmem[:, mb, (2 * mh + i) * 128:(2 * mh + i + 1) * 128], PTm[:, mb, :], mb == 0, mb == 1,
                       ["vmem", pk_(mb)], [pbk(b)])
                tt("dve", bfB[:, 2 * mh + i, :], pbank[b], rden, ALU.mult, [pbk(b), "rden"], [("bfB", 2 * mh + i)])
        if p3_stage < 4:
            continue
        for ft, b in branch_fm(None, lambda kt: bfB[:, kt, :], [("bfB", k) for k in range(8)], 8):
            tt("dve", hres[:, ft, :], hres[:, ft, :], pbank[b], ALU.add, HK + [pbk(b)], HK)
        if p3_stage < 5:
            continue
        ln_stage(32, 40, 32, 40, (bfA, "bfA"))
        if p3_stage < 6:
            continue
        for fb in range(8):
            wb, wbk = w_next()
            wb3 = wb.rearrange("p (a b) -> p a b", a=8)
            for q in range(4):
                fft = fb * 4 + q
                b = s.pb()
                for kt in range(8):
                    mm(pbank[b], wb3[:, kt, q * 128:(q + 1) * 128], bfA[:, kt, :], kt == 0, kt == 7, [wbk, "bfA"], [pbk(b)])
                sf = sqf[fft % 2]
                sfk = ("sqf", fft % 2)
                act(sf, pbank[b], AF.Square, [pbk(b)], [sfk])
                stt(aT[:, fft, :], pbank[b], 0.0, sf, ALU.is_gt, ALU.mult, [pbk(b), sfk], [("aT", fft)])
        if p3_stage < 7:
            continue
        for ft in range(8):
            wb, wbk = w_next()
            wb3 = wb.rearrange("p (a b) -> p a b", a=32)
            b = s.pb()
            for kt in range(32):
                mm(pbank[b], wb3[:, kt, :], aT[:, kt, :], kt == 0, kt == 31, [wbk, ("aT", kt)], [pbk(b)])
            tt("dve", hres[:, ft, :], hres[:, ft, :], pbank[b], ALU.add, HK + [pbk(b)], HK)
        if p3_stage < 8:
            continue
        ln_stage(0, 0, 0, 0, None, final_c=c)

    s.emit()
    return nc, dbg


def prep_shared(inp):
    f32 = np.float32
    w_in = np.asarray(inp["w_in"][0], f32)
    perm = np.r_[16:32, 0:16]
    ext = np.concatenate([w_in, np.zeros((1024, 64), f32), w_in[:, 2184:2216][:, perm]], axis=1)
    assert ext.shape[1] == WINC
    w_in_l = np.ascontiguousarray(ext.reshape(8, 128, WINC).transpose(1, 0, 2))
    wq = np.asarray(inp["w_q_up"][0], f32)
    wq2 = np.zeros((384, 8, 2, 96), f32)
    for h in range(8):
        wq2[:, h, 0, :] = wq[:, h * 96:(h + 1) * 96]
        wq2[:, h, 1, 64:96] = wq[:, h * 96 + 64 + perm]
    wq2 = np.ascontiguousarray(wq2.reshape(3, 128, 8, 2, 96).transpose(1, 2, 0, 3, 4)).reshape(128, 8 * 3 * 2 * 96)
    wkv = np.asarray(inp["w_kv_up"][0], f32)
    wkv2 = np.ascontiguousarray(wkv.reshape(2, 128, 1024).transpose(1, 0, 2)).reshape(128, 2048)

    def blk_cols(Wm, i):
        return Wm[:, 512 * i:512 * (i + 1)].reshape(8, 128, 512).transpose(1, 0, 2).reshape(128, 4096)

    blocks = []
    for nm in ("w_mix_out", "w_mem_q", "w_mem_o"):
        Wm = np.asarray(inp[nm][0], f32)
        blocks += [blk_cols(Wm, 0), blk_cols(Wm, 1)]
    Wu = np.asarray(inp["w_up"][0], f32)
    blocks += [blk_cols(Wu, i) for i in range(8)]
    Wd = np.asarray(inp["w_down"][0], f32)
    blocks += [Wd[:, 128 * i:128 * (i + 1)].reshape(32, 128, 128).transpose(1, 0, 2).reshape(128, 4096) for i in range(8)]
    for nm in ("w_mem_k", "w_mem_v"):
        Wm = np.asarray(inp[nm][0], f32)
        blocks += [blk_cols(Wm, 0), blk_cols(Wm, 1)]
    wblk = np.ascontiguousarray(np.stack(blocks, axis=0))
    assert wblk.shape == (NBLK, 128, 4096)

    def fcol(v):
        v = np.asarray(v, f32).reshape(-1)
        return v.reshape(-1, 128).T

    pcols = np.zeros((128, NPC), f32)
    pcols[:, 0:8] = fcol(inp["ln_in_g"])
    pcols[:, 8:16] = fcol(inp["ln_in_b"])
    pcols[:, 16:24] = fcol(inp["ln1_g"][0])
    pcols[:, 24:32] = fcol(inp["ln1_b"][0])
    pcols[:, 32:40] = fcol(inp["ln2_g"][0])
    pcols[:, 40:48] = fcol(inp["ln2_b"][0])
    cw = np.asarray(inp["conv_w"][0], f32)
    pcols[:, 48:80] = cw.reshape(4, 8, 128).transpose(2, 1, 0).reshape(128, 32)
    pcols[:, 80:88] = fcol(inp["conv_b"][0])
    pcols[:, 88:92] = fcol(inp["ssd_norm_g"][0])
    pcols[:, 92:95] = fcol(inp["q_norm_g"][0])
    pcols[:, 95:97] = fcol(inp["kv_norm_g"][0])
    half = 16
    inv_freq = np.power(np.float32(10000.0), -np.arange(half, dtype=f32) / np.float32(half)).astype(f32)
    pcols[64:96, 97] = np.concatenate([inv_freq, inv_freq])
    pcols[64:96, 98] = np.concatenate([-np.ones(16, f32), np.ones(16, f32)])
    prows = np.zeros((1, NPR), f32)
    prows[0, 0:8] = np.asarray(inp["dt_bias"][0], f32)
    prows[0, 8:16] = np.asarray(inp["a_log"][0], f32)
    prows[0, 16:24] = np.asarray(inp["d_skip"][0], f32)
    prows[0, 24:1048] = np.asarray(inp["ln3_g"][0], f32)
    prows[0, 1048:2072] = np.asarray(inp["ln3_b"][0], f32)
    cst = np.stack([np.eye(128), np.triu(np.ones((128, 128))), np.ones((128, 128))], axis=1).astype(f32)
    return {"w_in_l": w_in_l, "wq2": wq2, "wkv2": wkv2, "wblk": wblk, "pcols": pcols, "prows": prows,
            "cst": np.ascontiguousarray(cst)}


def make_in_maps(inp, cores):
    shared = prep_shared(inp)
    maps = []
    for b in cores:
        m = dict(shared)
        m["x"] = np.ascontiguousarray(np.asarray(inp["x"][b], np.float32))
        m["mem"] = np.ascontiguousarray(np.asarray(inp["mem"][b], np.float32))
        m["pos"] = np.ascontiguousarray(np.asarray(inp["positions"][b], np.int32).reshape(1, S))
        maps.append(m)
    return maps


def kernel(**inputs):
    nc, _ = build(debug=False)
    in_maps = make_in_maps(inputs, list(range(8)))
    res = run_bass_kernel_spmd(nc, in_maps, core_ids=list(range(8)))
    return np.stack([np.asarray(r["out"], np.float32) for r in res.results], axis=0)
```
